# Optimizing a Trainium2 kernel written in Bass

```python
import jax, jax.numpy as jnp
from jax import lax
import numpy as np

D_MODEL = 2048
BATCH = 8
SEQ = 4096
DEPTH = 4

N_MIXERS = 2
N_A = (DEPTH + 1) // 2
N_B = DEPTH // 2
HEAD_DIM = 128
N_HEADS = D_MODEL // HEAD_DIM
Q_BLOCK = 128
CHUNK = 128
D_GM = D_MODEL
GM_GROUP = 128
N_GM_GROUPS = D_GM // GM_GROUP
D_FF = 5504
CONV_W = 3
EPS = 1e-6

kernel_name = "hybrid_fox_gmlp_convffn_adaln"


def rms_norm(x, g):
    xf = x.astype(jnp.float32)
    y = xf * lax.rsqrt(jnp.mean(xf * xf, axis=-1, keepdims=True) + EPS)
    return (y * g.astype(jnp.float32)).astype(x.dtype)


def modulate(h, shift, scale):
    return h * (1 + scale[:, None, :]) + shift[:, None, :]


def fox_attention(h, w_in, b_f, w_o):
    B, S, D = h.shape
    proj = h @ w_in
    q = proj[..., :D].reshape(B, S, N_HEADS, HEAD_DIM).transpose(0, 2, 1, 3)
    k = proj[..., D:2 * D].reshape(B, S, N_HEADS, HEAD_DIM).transpose(0, 2, 1, 3)
    v = proj[..., 2 * D:3 * D].reshape(B, S, N_HEADS, HEAD_DIM).transpose(0, 2, 1, 3)
    f_logit = (proj[..., 3 * D:] + b_f).astype(jnp.float32)
    log_f = jax.nn.log_sigmoid(f_logit)
    F = jnp.cumsum(log_f, axis=1).transpose(0, 2, 1)
    scale = HEAD_DIM ** -0.5
    outs = []
    for i in range(S // Q_BLOCK):
        lo, hi = i * Q_BLOCK, (i + 1) * Q_BLOCK
        qb = q[:, :, lo:hi]
        kb = k[:, :, :hi]
        vb = v[:, :, :hi]
        s = jnp.einsum('bhqd,bhkd->bhqk', qb, kb).astype(jnp.float32) * scale
        s = s + F[:, :, lo:hi, None] - F[:, :, None, :hi]
        q_pos = lo + jnp.arange(Q_BLOCK)
        k_pos = jnp.arange(hi)
        mask = k_pos[None, :] <= q_pos[:, None]
        s = jnp.where(mask, s, -jnp.inf)
        p = jax.nn.softmax(s, axis=-1).astype(v.dtype)
        outs.append(jnp.einsum('bhqk,bhkd->bhqd', p, vb))
    o = jnp.concatenate(outs, axis=2)
    o = o.transpose(0, 2, 1, 3).reshape(B, S, D)
    return o @ w_o


def chunked_gmlp(h, w_in, v_g, w_s, b_s, w_o):
    B, S, _ = h.shape
    z = jax.nn.gelu(h @ w_in)
    u, v = z[..., :D_GM], z[..., D_GM:]
    v = rms_norm(v, v_g)
    vc = v.reshape(B, S // CHUNK, CHUNK, N_GM_GROUPS, GM_GROUP)
    causal = jnp.tril(jnp.ones((CHUNK, CHUNK), dtype=w_s.dtype))
    w = w_s * causal[None]
    sv = jnp.einsum('gts,bnsgd->bntgd', w, vc)
    sv = sv + b_s.T[None, None, :, :, None]
    gated = u * sv.reshape(B, S, D_GM)
    return gated @ w_o


def conv_ffn(h, w_in, conv_w, conv_b, w_out):
    a = h @ w_in
    S = a.shape[1]
    ap = jnp.pad(a, ((0, 0), (CONV_W - 1, 0), (0, 0)))
    a = (conv_w[0] * ap[:, 0:S] + conv_w[1] * ap[:, 1:S + 1]
         + conv_w[2] * ap[:, 2:S + 2] + conv_b)
    gate, up = a[..., :D_FF], a[..., D_FF:]
    return (jax.nn.silu(gate) * up) @ w_out


def setup_inputs(seed: int = 0) -> dict:
    key = jax.random.key(seed)
    ks = jax.random.split(key, 24)
    f32 = jnp.float32
    nrm = lambda k, shape, s: jax.random.normal(k, shape, f32) * s
    D = D_MODEL
    return {
        "x": nrm(ks[0], (BATCH, SEQ, D), 1.0),
        "c": nrm(ks[1], (BATCH, D), 1.0),
        "mod_w": nrm(ks[2], (DEPTH, D, 6 * D), 0.5 * D ** -0.5),
        "mod_b": nrm(ks[3], (DEPTH, 6 * D), 0.02),
        "mix_norm_g": 1.0 + nrm(ks[4], (DEPTH, D), 0.02),
        "ffn_norm_g": 1.0 + nrm(ks[5], (DEPTH, D), 0.02),
        "attn_w_in": nrm(ks[6], (N_A, D, 3 * D + N_HEADS), D ** -0.5),
        "attn_b_f": jax.random.uniform(ks[7], (N_A, N_HEADS), f32, 1.0, 6.0),
        "attn_w_o": nrm(ks[8], (N_A, D, D), D ** -0.5),
        "gm_w_in": nrm(ks[9], (N_B, D, 2 * D_GM), D ** -0.5),
        "gm_v_g": 1.0 + nrm(ks[10], (N_B, D_GM), 0.02),
        "gm_w_s": nrm(ks[11], (N_B, N_GM_GROUPS, CHUNK, CHUNK), CHUNK ** -0.5),
        "gm_b_s": 1.0 + nrm(ks[12], (N_B, N_GM_GROUPS, CHUNK), 0.1),
        "gm_w_o": nrm(ks[13], (N_B, D_GM, D), D_GM ** -0.5),
        "ffn_w_in": nrm(ks[14], (DEPTH, D, 2 * D_FF), D ** -0.5),
        "ffn_conv_w": nrm(ks[15], (DEPTH, CONV_W, 2 * D_FF), CONV_W ** -0.5),
        "ffn_conv_b": nrm(ks[16], (DEPTH, 2 * D_FF), 0.02),
        "ffn_w_out": nrm(ks[17], (DEPTH, D_FF, D), D_FF ** -0.5),
        "final_g": 1.0 + nrm(ks[18], (D,), 0.02),
    }


def reference(x, c, mod_w, mod_b, mix_norm_g, ffn_norm_g, attn_w_in, attn_b_f, attn_w_o,
              gm_w_in, gm_v_g, gm_w_s, gm_b_s, gm_w_o, ffn_w_in, ffn_conv_w, ffn_conv_b,
              ffn_w_out, final_g):
    c_act = jax.nn.silu(c)
    for i in range(DEPTH):
        mod = c_act @ mod_w[i] + mod_b[i]
        sh1, sc1, g1, sh2, sc2, g2 = jnp.split(mod, 6, axis=-1)
        h = modulate(rms_norm(x, mix_norm_g[i]), sh1, sc1)
        j = i // N_MIXERS
        if i % N_MIXERS == 0:
            y = fox_attention(h, attn_w_in[j], attn_b_f[j], attn_w_o[j])
        else:
            y = chunked_gmlp(h, gm_w_in[j], gm_v_g[j], gm_w_s[j], gm_b_s[j], gm_w_o[j])
        x = x + g1[:, None, :] * y
        h = modulate(rms_norm(x, ffn_norm_g[i]), sh2, sc2)
        x = x + g2[:, None, :] * conv_ffn(h, ffn_w_in[i], ffn_conv_w[i], ffn_conv_b[i], ffn_w_out[i])
    return rms_norm(x, final_g)
```

```python
import numpy as np
from contextlib import ExitStack
import concourse.bass as bass
import concourse.mybir as mybir
from concourse.bass_utils import run_bass_kernel_spmd

F32 = mybir.dt.float32
BF16 = mybir.dt.bfloat16
AF = mybir.ActivationFunctionType
ALU = mybir.AluOpType
AX = mybir.AxisListType

D = 2048
S = 4096
KC = 16
DFF = 5504
FC = 43
H = 16
DH = 128
NST = 2
TB = 512 * NST
NB = S // TB
EPS = 1e-6
NCORES = 8
SAME_ENGINE_SYNC = True


class Sem:
    def __init__(self, h, dma):
        self.h = h
        self.dma = dma
        self.issued = 0


class Tok:
    def __init__(self, name):
        self.name = name
        self.w = {}
        self.r = {}
        self.sem = None
        self.pend = None


class Eng:
    def __init__(self, q, sem, name, inorder_safe=False):
        self.q = q
        self.sem = sem
        self.name = name
        self.known = {}
        self.pending = []
        self.inorder_safe = inorder_safe


class Ctx:
    def __init__(self):
        self.nc = bass.Bass("TRN2", target_bir_lowering=False)
        self.es = ExitStack()
        self.sems = []
        self.nsem = 0
        nc = self.nc
        self.PE = Eng(nc.tensor, self.new_sem(False), "pe", inorder_safe=True)
        self.ACT = Eng(nc.scalar, self.new_sem(False), "act")
        self.DVE = Eng(nc.vector, self.new_sem(False), "dve")
        self.POOL = Eng(nc.gpsimd, self.new_sem(False), "pool")
        self.SP = Eng(nc.sync, None, "sp")
        self.engs = [self.PE, self.ACT, self.DVE, self.POOL, self.SP]

    def new_sem(self, dma):
        self.nsem += 1
        h = self.es.enter_context(self.nc.semaphore("s%d" % self.nsem))
        s = Sem(h, dma)
        self.sems.append(s)
        return s

    def _deps(self, eng, reads, writes, dis):
        ev = {}

        def add(d):
            for k, e in d.items():
                if k not in ev or ev[k][1] < e[1]:
                    ev[k] = e

        for t in reads:
            if t.pend is not None and t.pend is not eng:
                raise RuntimeError("token %s has pending unsignaled access" % t.name)
            add(t.w)
        for t in writes:
            if t.pend is not None and t.pend is not eng:
                raise RuntimeError("token %s has pending unsignaled access" % t.name)
            add(t.r)
            if not dis:
                add(t.w)
        return ev

    def _wait(self, eng, ev):
        for k, (sm, v) in ev.items():
            if sm is eng.sem and (eng.inorder_safe or not SAME_ENGINE_SYNC):
                continue
            val = sm.issued if sm.dma else v
            if eng.known.get(k, 0) >= val:
                continue
            eng.q.wait_ge(sm.h, val)
            eng.known[k] = val

    @staticmethod
    def _record(event, reads, writes, dis):
        k = id(event[0])
        for t in reads:
            t.r[k] = event
        for t in writes:
            if dis:
                t.w[k] = event
            else:
                t.w = {k: event}
                t.r = {}

    def op(self, eng, fn, reads=(), writes=(), signal=True, dis=False):
        ev = self._deps(eng, reads, writes, dis)
        self._wait(eng, ev)
        inst = fn()
        if signal:
            eng.sem.issued += 1
            inst.then_inc(eng.sem.h, 1)
            event = (eng.sem, eng.sem.issued)
            for (r, w, d) in eng.pending:
                self._record(event, r, w, d)
                for t in list(r) + list(w):
                    t.pend = None
            eng.pending = []
            self._record(event, reads, writes, dis)
        else:
            eng.pending.append((tuple(reads), tuple(writes), dis))
            for t in list(reads) + list(writes):
                t.pend = eng
        return inst

    def dma(self, eng, out, in_, sbtok, reads=(), writes=(), dis=False):
        ev = self._deps(eng, reads, writes, dis)
        self._wait(eng, ev)
        if sbtok.sem is None:
            sbtok.sem = self.new_sem(True)
        sm = sbtok.sem
        eng.q.dma_start(out=out, in_=in_).then_inc(sm.h, 16)
        sm.issued += 16
        self._record((sm, sm.issued), reads, writes, dis)

    def barrier(self):
        for e in self.engs:
            assert not e.pending, "pending ops at barrier on %s" % e.name
        for e in self.engs:
            for sm in self.sems:
                if sm.issued == 0:
                    continue
                k = id(sm)
                if e.known.get(k, 0) >= sm.issued:
                    continue
                if sm is e.sem and e.inorder_safe:
                    continue
                e.q.wait_ge(sm.h, sm.issued)
                e.known[k] = sm.issued


class Rot:
    def __init__(self, items):
        self.items = items
        self.i = 0

    def next(self):
        it = self.items[self.i % len(self.items)]
        self.i += 1
        return it


class WStream:
    def __init__(self, slots, loads):
        self.slots = slots
        self.loads = loads
        self.views = [None] * len(loads)
        self.issued = 0

    def get(self, i):
        ns = len(self.slots)
        lim = min(len(self.loads), i + ns)
        while self.issued < lim:
            j = self.issued
            ap, tok = self.slots[j % ns]
            self.views[j] = (self.loads[j](ap, tok), tok)
            self.issued += 1
        return self.views[i]


def step(gen, n=1):
    if gen is None:
        return
    for _ in range(n):
        try:
            next(gen)
        except StopIteration:
            return


def exhaust(gen):
    if gen is None:
        return
    for _ in gen:
        pass


def build(n_layers=4, do_final=True, stop_after=None):
    K = Ctx()
    nc = K.nc
    PE, ACT, DVE, POOL, SP = K.PE, K.ACT, K.DVE, K.POOL, K.SP
    op, dma = K.op, K.dma

    def din(name, shape, dt=F32):
        return nc.dram_tensor(name, list(shape), dt, kind="ExternalInput").ap()

    xT_in = din("xT", [D, S])
    c_t = din("c_t", [128, KC])
    mod_w = din("mod_w", [4, D, 6 * D])
    mod_b = din("mod_b", [128, 4 * 96])
    mixg = din("mixg", [128, 4 * KC])
    ffng = din("ffng", [128, 4 * KC])
    a_win = din("a_win", [2, D, 3 * D + H])
    a_bf = din("a_bf", [2, H, 1])
    a_wo = din("a_wo", [2, D, D])
    g_win = din("g_win", [2, D, 2 * D])
    g_vg = din("g_vg", [2, 1, D])
    g_ws = din("g_ws", [2, 128, H * 128])
    g_bs = din("g_bs", [2, 1, D])
    g_wo = din("g_wo", [2, D, D])
    f_win = din("f_win", [4, D, 2 * DFF])
    f_cw = din("f_cw", [4, 128, 3 * 2 * FC])
    f_cb = din("f_cb", [4, 128, 2 * FC])
    f_wout = din("f_wout", [4, DFF, D])
    fin_g = din("fin_g", [128, KC])
    c_ident = din("c_ident", [128, 128])
    c_masks = din("c_masks", [128, 4 * 512])
    c_tri = din("c_tri", [128, H * 128])
    outT = nc.dram_tensor("outT", [D, S], F32, kind="ExternalOutput").ap()

    XT = nc.dram_tensor("XT", [D, S], F32).ap()
    QT = nc.dram_tensor("QT", [D, S], BF16).ap()
    KT = nc.dram_tensor("KT", [D, S], BF16).ap()
    OT = nc.dram_tensor("OT", [D, S], BF16).ap()
    VV = nc.dram_tensor("VV", [S, D], BF16).ap()
    FR = nc.dram_tensor("FR", [6, H, S], BF16).ap()

    XT_tok = [Tok("XT%d" % i) for i in range(S // 512)]
    QT_tok, KT_tok, VV_tok, OT_tok, FR_tok = Tok("QT"), Tok("KT"), Tok("VV"), Tok("OT"), Tok("FR")
    NOTOK = Tok("ro")
    _gt = {}
    _uid = [0]

    def sbt(name, shape, dt):
        _uid[0] += 1
        return nc.sbuf_tensor("%s_u%d" % (name, _uid[0]), shape, dt)

    def GTok(name):
        if name not in _gt:
            _gt[name] = Tok(name)
        return _gt[name]

    es = K.es

    def sb(name, shape, dt):
        return es.enter_context(nc.sbuf_tensor(name, list(shape), dt))

    HT = sb("HT", [128, KC, TB], BF16)
    HT_tok = [Tok("HT%d" % i) for i in range(NST)]
    nx_rot = Rot([(sb("nx%d" % i, [128, 512], F32), Tok("nx%d" % i)) for i in range(4)])
    xt_rot = Rot([(sb("xt%d" % i, [128, 512], F32), Tok("xt%d" % i)) for i in range(3)])
    sq_rot = Rot([(sb("sq%d" % i, [128, 512], BF16), Tok("sq%d" % i)) for i in range(2)])
    rstd_rot = Rot([(sb("rstd%d" % i, [128, 512], F32), Tok("rstd%d" % i)) for i in range(1)])
    modall = sb("modall", [128, 4 * 96], F32)
    modb_sb = sb("modb_sb", [128, 4 * 96], F32)
    gsc1all = sb("gsc1all", [128, 4 * KC], F32)
    gsc2all = sb("gsc2all", [128, 4 * KC], F32)
    mixg_sb = sb("mixg_sb", [128, 4 * KC], F32)
    ffng_sb = sb("ffng_sb", [128, 4 * KC], F32)
    fing_sb = sb("fing_sb", [128, KC], F32)
    cin = sb("cin", [128, KC], F32)
    cact = sb("cact", [128, KC], BF16)
    ident = sb("ident", [128, 128], BF16)
    ones = sb("ones", [128, 128], BF16)
    VECL = [Tok("vec%d" % i) for i in range(4)]
    CONST = Tok("const")
    PS = [es.enter_context(nc.psum_tensor("ps%d" % i, [128, 512], F32)) for i in range(8)]
    PS_tok = [Tok("ps%d" % i) for i in range(8)]
    ps_rr = [0]

    def next_ps():
        i = ps_rr[0] % 7
        ps_rr[0] += 1
        return PS[i], PS_tok[i]

    dma(POOL, ident[:], c_ident, CONST, writes=[CONST], dis=True)
    op(DVE, lambda: nc.vector.memset(ones[:], 1.0), writes=[CONST], dis=True)
    dma(SP, cin[:], c_t, CONST, writes=[CONST], dis=True)
    dma(SP, fing_sb[:], fin_g, CONST, writes=[CONST], dis=True)
    dma(SP, modb_sb[:], mod_b, CONST, writes=[CONST], dis=True)
    dma(SP, mixg_sb[:], mixg, CONST, writes=[CONST], dis=True)
    dma(SP, ffng_sb[:], ffng, CONST, writes=[CONST], dis=True)
    op(ACT, lambda: nc.scalar.activation(out=cact[:], in_=cin[:], func=AF.Silu), reads=[CONST], writes=[CONST], dis=True)
    K.barrier()

    def wview(w2d):
        return w2d.rearrange("(kc p) n -> p kc n", p=128)

    def load_panel(dst3, tok, wv, kc0, kc1, n0, n1):
        k = kc0
        while k < kc1:
            ke = min(k + 16, kc1)
            dma(POOL, dst3[:, k - kc0:ke - kc0, 0:n1 - n0], wv[:, k:ke, n0:n1], tok, writes=[tok], dis=(k > kc0))
            k = ke
        return dst3

    def panel_loads(wv, nk, cols):
        return [(lambda ap, tok, n0=n0: load_panel(ap, tok, wv, 0, nk, n0, n0 + 512)) for n0 in cols]

    PF = 2

    def norm_gen(xsrc, xsrc_toks, blk, gcol, bcol, vtok, HTd=None, HTd_tok=None, to_out=False, rot=None, PF=2):
        pss, pss_tok = PS[7], PS_tok[7]
        seq = [(st, p, c) for st in range(NST) for p in (1, 2) for c in range(KC)]
        tiles = {}

        def ld(k):
            st, p, c = seq[k]
            t0 = (blk * NST + st) * 512
            xt, xtok = (rot or nx_rot).next()
            dma(SP, xt[:], xsrc[c * 128:(c + 1) * 128, t0:t0 + 512], xtok, reads=[xsrc_toks[blk * NST + st]], writes=[xtok])
            tiles[k] = (xt, xtok)

        for k in range(min(PF, len(seq))):
            ld(k)
        yield
        rs = rstok = None
        for k, (st, p, c) in enumerate(seq):
            if k + PF < len(seq):
                ld(k + PF)
            t0 = (blk * NST + st) * 512
            xt, xtok = tiles.pop(k)
            if p == 1:
                sq, sqtok = sq_rot.next()
                op(ACT, lambda: nc.scalar.activation(out=sq[:], in_=xt[:], func=AF.Square), reads=[xtok], writes=[sqtok])
                op(PE, lambda: nc.tensor.matmul(pss[:], lhsT=ones[:, 0:128], rhs=sq[:], start=(c == 0), stop=(c == KC - 1)),
                   reads=[sqtok], writes=[pss_tok], signal=True)
                if c == KC - 1:
                    rs, rstok = rstd_rot.next()
                    op(DVE, lambda: nc.vector.tensor_scalar(out=rs[:], in0=pss[:], scalar1=1.0 / D, scalar2=EPS, op0=ALU.mult, op1=ALU.add),
                       reads=[pss_tok], writes=[rstok])
                    op(DVE, lambda: nc.vector.reciprocal(out=rs[:], in_=rs[:]), reads=[rstok], writes=[rstok])
                    op(ACT, lambda: nc.scalar.activation(out=rs[:], in_=rs[:], func=AF.Sqrt), reads=[rstok], writes=[rstok])
            else:
                op(DVE, lambda: nc.vector.tensor_tensor(out=xt[:], in0=xt[:], in1=rs[:], op=ALU.mult), reads=[xtok, rstok], writes=[xtok])
                if to_out:
                    op(ACT, lambda: nc.scalar.activation(out=xt[:], in_=xt[:], func=AF.Identity, scale=gcol[:, c:c + 1]),
                       reads=[xtok], writes=[xtok])
                    dma(SP, outT[c * 128:(c + 1) * 128, t0:t0 + 512], xt[:], xtok, reads=[xtok])
                else:
                    op(ACT, lambda: nc.scalar.activation(out=HTd[:, c, st * 512:(st + 1) * 512], in_=xt[:], func=AF.Identity,
                                                         scale=gcol[:, c:c + 1], bias=bcol[:, c:c + 1]),
                       reads=[xtok, vtok], writes=[HTd_tok[st]], dis=True)
            yield

    xrot_cur = [xt_rot]

    def extra_xt(ph, n):
        items = list(xt_rot.items) + [(ph.enter_context(sbt("xtx%d" % i, [128, 512], F32)), GTok("xtx%d" % i)) for i in range(n)]
        xrot_cur[0] = Rot(items)

    def resid_load(n, sg, xsrc, xsrc_toks):
        t0 = sg * 512
        xt, xtok = xrot_cur[0].next()
        dma(SP, xt[:], xsrc[n * 128:(n + 1) * 128, t0:t0 + 512], xtok, reads=[xsrc_toks[sg]], writes=[xtok])
        return xt, xtok

    def resid_apply(ps, pstok, xtile, n, sg, gcolumn, vtok):
        t0 = sg * 512
        xt, xtok = xtile
        op(DVE, lambda: nc.vector.scalar_tensor_tensor(out=xt[:], in0=ps[:], scalar=gcolumn, in1=xt[:], op0=ALU.mult, op1=ALU.add),
           reads=[pstok, xtok, vtok], writes=[xtok])
        dma(SP, XT[n * 128:(n + 1) * 128, t0:t0 + 512], xt[:], xtok, reads=[xtok], writes=[XT_tok[sg]], dis=True)

    def out_proj(ws, wbase, blk, rhs3, rhs_toks, nk, gcols, vtok, xsrc, xsrc_toks, bg=None, bg_n=1):
        groups = [(pn, q, st) for pn in range(D // 512) for q in range(4) for st in range(NST)]
        nxt = resid_load(groups[0][0] * 4 + groups[0][1], blk * NST + groups[0][2], xsrc, xsrc_toks)
        for gi, (pn, q, st) in enumerate(groups):
            p3, ptok = ws.get(wbase + pn)
            n = pn * 4 + q
            cur = nxt
            if gi + 1 < len(groups):
                pn2, q2, st2 = groups[gi + 1]
                nxt = resid_load(pn2 * 4 + q2, blk * NST + st2, xsrc, xsrc_toks)
            ps, pstok = next_ps()
            for kc in range(nk):
                op(PE, lambda: nc.tensor.matmul(ps[:], lhsT=p3[:, kc, q * 128:(q + 1) * 128],
                                                rhs=rhs3[:, kc, st * 512:(st + 1) * 512],
                                                start=(kc == 0), stop=(kc == nk - 1)),
                   reads=[ptok, rhs_toks[st]], writes=[pstok], signal=(kc == nk - 1))
            step(bg, bg_n)
            resid_apply(ps, pstok, cur, n, blk * NST + st, gcols[:, n:n + 1], vtok)

    def mod_gen(li, slots):
        wv = wview(mod_w[li])
        psm, psm_tok = PS[7], PS_tok[7]
        ws = WStream(slots, panel_loads(wv, KC, [pn * 512 for pn in range(6 * D // 512)]))
        for pn in range(6 * D // 512):
            p3, ptok = ws.get(pn)
            for q in range(4):
                n = pn * 4 + q
                for kc in range(KC):
                    op(PE, lambda: nc.tensor.matmul(psm[:, n:n + 1], lhsT=p3[:, kc, q * 128:(q + 1) * 128], rhs=cact[:, kc:kc + 1],
                                                    start=(kc == 0), stop=(kc == KC - 1)),
                       reads=[ptok], writes=[psm_tok], signal=(kc == KC - 1))
                yield
        m0 = li * 96
        vt = VECL[li]
        op(DVE, lambda: nc.vector.tensor_tensor(out=modall[:, m0:m0 + 96], in0=psm[:, 0:96], in1=modb_sb[:, m0:m0 + 96], op=ALU.add),
           reads=[psm_tok], writes=[vt])
        op(DVE, lambda: nc.vector.scalar_tensor_tensor(out=gsc1all[:, li * KC:(li + 1) * KC], in0=modall[:, m0 + 16:m0 + 32], scalar=1.0,
                                                       in1=mixg_sb[:, li * KC:(li + 1) * KC], op0=ALU.add, op1=ALU.mult),
           reads=[vt], writes=[vt])
        op(DVE, lambda: nc.vector.scalar_tensor_tensor(out=gsc2all[:, li * KC:(li + 1) * KC], in0=modall[:, m0 + 64:m0 + 80], scalar=1.0,
                                                       in1=ffng_sb[:, li * KC:(li + 1) * KC], op0=ALU.add, op1=ALU.mult),
           reads=[vt], writes=[vt])
        yield

    xsrc, xsrc_toks = xT_in, [NOTOK] * (S // 512)
    XTL = [XT, XT_tok]
    mod_done = set()
    prenorm = [False]

    def first_norm(gcol, bcol, vtok_):
        if prenorm[0]:
            prenorm[0] = False
            return
        exhaust(norm_gen(xsrc, xsrc_toks, 0, gcol, bcol, vtok_, HT, HT_tok))

    def next_phase_norm(gcol, bcol, vtok_):
        prenorm[0] = True
        return norm_gen(XT, XT_tok, 0, gcol, bcol, vtok_, HT, HT_tok)

    def chain(gens):
        for g in gens:
            yield from g

    for li in range(n_layers):
        j = li // 2
        if li not in mod_done:
            with ExitStack() as ph:
                slots = [(ph.enter_context(sbt("mpan%d" % i, [128, KC, 512], BF16)), GTok("mpan%d" % i)) for i in range(3)]
                exhaust(mod_gen(li, slots))
                mod_done.add(li)
                K.barrier()
        vt = VECL[li]
        m0 = li * 96
        sh1 = modall[:, m0 + 0:m0 + 16]
        g1c = modall[:, m0 + 32:m0 + 48]
        sh2 = modall[:, m0 + 48:m0 + 64]
        g2c = modall[:, m0 + 80:m0 + 96]
        gsc1 = gsc1all[:, li * KC:(li + 1) * KC]
        gsc2 = gsc2all[:, li * KC:(li + 1) * KC]

        if li % 2 == 0:
            with ExitStack() as ph:
                fpan = ph.enter_context(sbt("fpan", [128, KC, H], BF16))
                fpan_tok = GTok("fpan")
                FN = ph.enter_context(sbt("FN", [H, S], F32))
                FN_tok = GTok("FN")
                ftmp = ph.enter_context(sbt("ftmp", [H, 512], F32))
                ftmp_tok = GTok("ftmp")
                nbf = ph.enter_context(sbt("nbf", [H, 1], F32))
                nbf_tok = GTok("nbf")
                pha = ExitStack()
                slots = [(pha.enter_context(sbt("qpan%d" % i, [128, KC, 512], BF16)), GTok("pan%d" % i)) for i in range(3)]
                HT2 = pha.enter_context(sbt("HT2", [128, KC, TB], BF16))
                HT2_tok = [GTok("HT2_%d" % i) for i in range(NST)]
                HTs = [(HT, HT_tok), (HT2, HT2_tok)]
                stg_rot = Rot([(pha.enter_context(sbt("stg%d" % i, [128, 512], BF16)), GTok("stg%d" % i)) for i in range(4)])
                wv = wview(a_win[j])
                ws = WStream(slots, panel_loads(wv, KC, [pn * 512 for pn in range(12)] * NB))
                ws.get(0)
                dma(POOL, fpan[:], wv[:, :, 3 * D:3 * D + H], fpan_tok, writes=[fpan_tok])
                dma(SP, nbf[:], a_bf[j], nbf_tok, writes=[nbf_tok])
                op(DVE, lambda: nc.vector.tensor_scalar(out=nbf[:], in0=nbf[:], scalar1=-1.0, scalar2=None, op0=ALU.mult),
                   reads=[nbf_tok], writes=[nbf_tok])
                first_norm(gsc1, sh1, vt)
                for blk in range(NB):
                    Hc, Hc_tok = HTs[blk % 2]
                    bg = None
                    if blk + 1 < NB:
                        Hn, Hn_tok = HTs[(blk + 1) % 2]
                        bg = norm_gen(xsrc, xsrc_toks, blk + 1, gsc1, sh1, vt, Hn, Hn_tok)
                    for pn in range(12):
                        p3, ptok = ws.get(blk * 12 + pn)
                        if pn < 8:
                            for q in range(4):
                                n = (pn % 4) * 4 + q
                                for st in range(NST):
                                    ps, pstok = next_ps()
                                    for kc in range(KC):
                                        op(PE, lambda: nc.tensor.matmul(ps[:], lhsT=p3[:, kc, q * 128:(q + 1) * 128],
                                                                        rhs=Hc[:, kc, st * 512:(st + 1) * 512],
                                                                        start=(kc == 0), stop=(kc == KC - 1)),
                                           reads=[ptok, Hc_tok[st]], writes=[pstok], signal=(kc == KC - 1))
                                    sg_, sgtok = stg_rot.next()
                                    t0 = (blk * NST + st) * 512
                                    if pn < 4:
                                        op(ACT, lambda: nc.scalar.activation(out=sg_[:], in_=ps[:], func=AF.Copy, scale=float(DH ** -0.5)),
                                           reads=[pstok], writes=[sgtok])
                                        dma(SP, QT[n * 128:(n + 1) * 128, t0:t0 + 512], sg_[:], sgtok, reads=[sgtok], writes=[QT_tok], dis=True)
                                    else:
                                        op(DVE, lambda: nc.vector.tensor_copy(out=sg_[:], in_=ps[:]), reads=[pstok], writes=[sgtok])
                                        dma(SP, KT[n * 128:(n + 1) * 128, t0:t0 + 512], sg_[:], sgtok, reads=[sgtok], writes=[KT_tok], dis=True)
                                    step(bg)
                        else:
                            cc = pn - 8
                            for tt in range(TB // 128):
                                st = tt // 4
                                ps, pstok = next_ps()
                                for kc in range(KC):
                                    op(PE, lambda: nc.tensor.matmul(ps[:], lhsT=Hc[:, kc, tt * 128:(tt + 1) * 128], rhs=p3[:, kc, :],
                                                                    start=(kc == 0), stop=(kc == KC - 1)),
                                       reads=[ptok, Hc_tok[st]], writes=[pstok], signal=(kc == KC - 1))
                                sg_, sgtok = stg_rot.next()
                                tk0 = blk * TB + tt * 128
                                if tt % 2 == 0:
                                    op(ACT, lambda: nc.scalar.activation(out=sg_[:], in_=ps[:], func=AF.Copy), reads=[pstok], writes=[sgtok])
                                else:
                                    op(DVE, lambda: nc.vector.tensor_copy(out=sg_[:], in_=ps[:]), reads=[pstok], writes=[sgtok])
                                dma(SP, VV[tk0:tk0 + 128, cc * 512:(cc + 1) * 512], sg_[:], sgtok, reads=[sgtok], writes=[VV_tok], dis=True)
                                step(bg)
                    for st in range(NST):
                        ps, pstok = next_ps()
                        for kc in range(KC):
                            op(PE, lambda: nc.tensor.matmul(ps[0:H, :], lhsT=fpan[:, kc, :], rhs=Hc[:, kc, st * 512:(st + 1) * 512],
                                                            start=(kc == 0), stop=(kc == KC - 1)),
                               reads=[fpan_tok, Hc_tok[st]], writes=[pstok], signal=(kc == KC - 1))
                        t0 = (blk * NST + st) * 512
                        op(ACT, lambda: nc.scalar.activation(out=ftmp[:], in_=ps[0:H, :], func=AF.Exp, scale=-1.0, bias=nbf[:, 0:1]),
                           reads=[pstok, nbf_tok], writes=[ftmp_tok])
                        op(ACT, lambda: nc.scalar.activation(out=FN[:, t0:t0 + 512], in_=ftmp[:], func=AF.Ln, bias=1.0),
                           reads=[ftmp_tok], writes=[FN_tok], dis=True)
                    exhaust(bg)
                K.barrier()
                pha.close()
                G = ph.enter_context(sbt("G", [H, S], F32))
                R1 = ph.enter_context(sbt("R1", [H, S], F32))
                spl = [ph.enter_context(sbt("spl%d" % i, [H, S], BF16)) for i in range(6)]
                GT = GTok("G")
                op(DVE, lambda: nc.vector.tensor_tensor_scan(out=G[:], data0=FN[:], data1=FN[:], initial=0.0, op0=ALU.add, op1=ALU.bypass),
                   reads=[FN_tok], writes=[GT])
                op(DVE, lambda: nc.vector.tensor_copy(out=spl[0][:], in_=G[:]), reads=[GT], writes=[GT], dis=True)
                op(DVE, lambda: nc.vector.tensor_tensor(out=R1[:], in0=G[:], in1=spl[0][:], op=ALU.subtract), reads=[GT], writes=[GT], dis=True)
                op(DVE, lambda: nc.vector.tensor_copy(out=spl[1][:], in_=R1[:]), reads=[GT], writes=[GT], dis=True)
                op(DVE, lambda: nc.vector.tensor_tensor(out=G[:], in0=R1[:], in1=spl[1][:], op=ALU.subtract), reads=[GT], writes=[GT], dis=True)
                op(DVE, lambda: nc.vector.tensor_copy(out=spl[2][:], in_=G[:]), reads=[GT], writes=[GT], dis=True)
                for i in range(3):
                    op(DVE, lambda: nc.vector.tensor_scalar(out=spl[3 + i][:], in0=spl[i][:], scalar1=-1.0, scalar2=None, op0=ALU.mult),
                       reads=[GT], writes=[GT], dis=True)
                for i in range(6):
                    dma(SP, FR[i], spl[i][:], GT, reads=[GT], writes=[FR_tok], dis=True)
                K.barrier()

            with ExitStack() as ph:
                sets = []
                for i in range(2):
                    d = dict(
                        q=ph.enter_context(sbt("aq%d" % i, [128, S], BF16)),
                        k=ph.enter_context(sbt("ak%d" % i, [128, S], BF16)),
                        v=ph.enter_context(sbt("av%d" % i, [128, S // 128, DH], BF16)),
                        rk=ph.enter_context(sbt("ark%d" % i, [6, S], BF16)),
                        rq=ph.enter_context(sbt("arq%d" % i, [6, S], BF16)),
                        tok=GTok("aset%d" % i))
                    sets.append(d)
                masks = ph.enter_context(sbt("masks", [128, 4, 512], BF16))
                mtok = GTok("masks")
                dma(POOL, masks[:], c_masks.rearrange("p (a b) -> p a b", a=4), mtok, writes=[mtok])
                pt_rot = Rot([(ph.enter_context(sbt("pt%d" % i, [128, 512], BF16)), GTok("pt%d" % i)) for i in range(4)])
                rl_rot = Rot([(ph.enter_context(sbt("rl%d" % i, [128, 512], F32)), GTok("rl%d" % i)) for i in range(2)])
                stg_rot = Rot([(ph.enter_context(sbt("stg%d" % i, [128, 512], BF16)), GTok("stg%d" % i)) for i in range(4)])
                mslots = [(ph.enter_context(sbt("mpan%d" % i, [128, KC, 512], BF16)), GTok("mpan%d" % i)) for i in range(3)]
                todo = [l for l in (li + 1, li + 2) if l < n_layers and l not in mod_done]
                bg = chain([mod_gen(l, mslots) for l in todo]) if todo else None
                bg_n = 2 if len(todo) > 1 else 1
                for l in todo:
                    mod_done.add(l)
                for d in sets:
                    op(DVE, lambda: nc.vector.memset(d["rk"][:], 1.0), writes=[d["tok"]], dis=True)
                    op(DVE, lambda: nc.vector.memset(d["rq"][:], 1.0), writes=[d["tok"]], dis=True)
                vview = VV.rearrange("(kt p) d -> p kt d", p=128)

                def load_head(h, d):
                    tk = d["tok"]
                    dma(SP, d["q"][:], QT[h * 128:(h + 1) * 128, :], tk, reads=[QT_tok], writes=[tk], dis=False)
                    dma(SP, d["k"][:], KT[h * 128:(h + 1) * 128, :], tk, reads=[KT_tok], writes=[tk], dis=True)
                    dma(SP, d["v"][:, 0:16, :], vview[:, 0:16, h * 128:(h + 1) * 128], tk, reads=[VV_tok], writes=[tk], dis=True)
                    dma(SP, d["v"][:, 16:32, :], vview[:, 16:32, h * 128:(h + 1) * 128], tk, reads=[VV_tok], writes=[tk], dis=True)
                    dma(SP, d["rk"][0:3, :], FR[0:3, h, :], tk, reads=[FR_tok], writes=[tk], dis=True)
                    dma(SP, d["rq"][3:6, :], FR[3:6, h, :], tk, reads=[FR_tok], writes=[tk], dis=True)

                load_head(0, sets[0])
                SB_ = [0, 1, 2]
                for h in range(H):
                    d = sets[h % 2]
                    if h + 1 < H:
                        load_head(h + 1, sets[(h + 1) % 2])
                    tk = d["tok"]
                    for Q in range(S // 512):
                        nk = 4 * Q + 4
                        po, potok = PS[3 + Q % 2], PS_tok[3 + Q % 2]
                        pl, pltok = PS[5 + Q % 2], PS_tok[5 + Q % 2]
                        qs = slice(Q * 512, (Q + 1) * 512)

                        def qk(jk):
                            ps, pstok = PS[SB_[jk % 3]], PS_tok[SB_[jk % 3]]
                            ks = slice(jk * 128, (jk + 1) * 128)
                            diag = jk >= 4 * Q
                            op(PE, lambda: nc.tensor.matmul(ps[:], lhsT=d["k"][:, ks], rhs=d["q"][:, qs], start=True, stop=False),
                               reads=[tk], writes=[pstok], signal=False)
                            op(PE, lambda: nc.tensor.matmul(ps[:], lhsT=d["rk"][0:6, ks], rhs=d["rq"][0:6, qs], start=False, stop=(not diag)),
                               reads=[tk], writes=[pstok], signal=(not diag))
                            if diag:
                                op(PE, lambda: nc.tensor.matmul(ps[:], lhsT=ident[:, :], rhs=masks[:, jk - 4 * Q, :], start=False, stop=True),
                                   reads=[mtok], writes=[pstok], signal=True)

                        qk(0)
                        for jk in range(nk):
                            if jk + 1 < nk:
                                qk(jk + 1)
                            ps, pstok = PS[SB_[jk % 3]], PS_tok[SB_[jk % 3]]
                            pt, pttok = pt_rot.next()
                            op(ACT, lambda: nc.scalar.activation(out=pt[:], in_=ps[:], func=AF.Exp), reads=[pstok], writes=[pttok])
                            op(PE, lambda: nc.tensor.matmul(po[:], lhsT=d["v"][:, jk, :], rhs=pt[:], start=(jk == 0), stop=(jk == nk - 1)),
                               reads=[tk, pttok], writes=[potok], signal=False)
                            op(PE, lambda: nc.tensor.matmul(pl[:], lhsT=ones[:, :], rhs=pt[:], start=(jk == 0), stop=(jk == nk - 1)),
                               reads=[pttok], writes=[pltok], signal=True)
                        rl, rltok = rl_rot.next()
                        op(DVE, lambda: nc.vector.reciprocal(out=rl[:], in_=pl[:]), reads=[pltok], writes=[rltok])
                        sg_, sgtok = stg_rot.next()
                        op(DVE, lambda: nc.vector.tensor_tensor(out=sg_[:], in0=po[:], in1=rl[:], op=ALU.mult),
                           reads=[potok, rltok], writes=[sgtok])
                        dma(SP, OT[h * 128:(h + 1) * 128, qs], sg_[:], sgtok, reads=[sgtok], writes=[OT_tok], dis=True)
                        step(bg, bg_n)
                exhaust(bg)
                K.barrier()

            with ExitStack() as ph:
                slots = [(ph.enter_context(sbt("opan%d" % i, [128, KC, 512], BF16)), GTok("opan%d" % i)) for i in range(4)]
                HT2 = ph.enter_context(sbt("HT2", [128, KC, TB], BF16))
                HT2_tok = [GTok("HT2_%d" % i) for i in range(NST)]
                HTs = [(HT, HT_tok), (HT2, HT2_tok)]
                extra_xt(ph, 5)
                otv = OT.rearrange("(kc p) t -> p kc t", p=128)

                class _Res:
                    def __init__(self):
                        wvo_ = wview(a_wo[j])
                        self.v = [(load_panel(ap, tok, wvo_, 0, KC, pn * 512, (pn + 1) * 512), tok) for pn, (ap, tok) in enumerate(slots)]

                    def get(self, i):
                        return self.v[i % 4]

                ws = _Res()

                def load_o(blk):
                    Hd, Hd_tok = HTs[blk % 2]
                    for st in range(NST):
                        t0 = (blk * NST + st) * 512
                        dma(SP, Hd[:, :, st * 512:(st + 1) * 512], otv[:, :, t0:t0 + 512], Hd_tok[st], reads=[OT_tok], writes=[Hd_tok[st]])

                load_o(0)
                for blk in range(NB):
                    if blk + 1 < NB:
                        load_o(blk + 1)
                    Hc, Hc_tok = HTs[blk % 2]
                    bgn = None
                    if blk == NB - 1 and Hc is not HT:
                        bgn = next_phase_norm(gsc2, sh2, vt)
                    out_proj(ws, blk * 4, blk, Hc, Hc_tok, KC, g1c, vt, xsrc, xsrc_toks, bg=bgn, bg_n=2)
                    exhaust(bgn)
                K.barrier()
                xrot_cur[0] = xt_rot
        else:
            with ExitStack() as ph:
                wsT = ph.enter_context(sbt("wsT", [128, H * 128], BF16))
                wsT_tok = GTok("wsT")
                vgb = ph.enter_context(sbt("vgb", [128, D], F32))
                vgb_tok = GTok("vgb")
                bhi = ph.enter_context(sbt("bhi", [1, D], BF16))
                blo = ph.enter_context(sbt("blo", [1, D], BF16))
                bs_tok = GTok("bs")
                with ExitStack() as ph2:
                    bsf = ph2.enter_context(sbt("bsf", [1, D], F32))
                    wsf = ph2.enter_context(sbt("wsf", [128, H * 128], F32))
                    tri = ph2.enter_context(sbt("tri", [128, H * 128], F32))
                    dma(SP, wsf[:], g_ws[j], wsT_tok, writes=[wsT_tok])
                    dma(SP, tri[:], c_tri, wsT_tok, writes=[wsT_tok], dis=True)
                    op(DVE, lambda: nc.vector.tensor_tensor(out=wsT[:], in0=wsf[:], in1=tri[:], op=ALU.mult), reads=[wsT_tok], writes=[wsT_tok], dis=True)
                    dma(SP, vgb[:], g_vg[j].partition_broadcast(128), vgb_tok, writes=[vgb_tok])
                    dma(SP, bsf[:], g_bs[j], bs_tok, writes=[bs_tok])
                    op(DVE, lambda: nc.vector.tensor_copy(out=bhi[:], in_=bsf[:]), reads=[bs_tok], writes=[bs_tok], dis=True)
                    op(DVE, lambda: nc.vector.tensor_tensor(out=bsf[:], in0=bsf[:], in1=bhi[:], op=ALU.subtract), reads=[bs_tok], writes=[bs_tok], dis=True)
                    op(DVE, lambda: nc.vector.tensor_copy(out=blo[:], in_=bsf[:]), reads=[bs_tok], writes=[bs_tok], dis=True)
                    K.barrier()
                slots = [(ph.enter_context(sbt("gpan%d" % i, [128, KC, 512], BF16)), GTok("pan%d" % i)) for i in range(3)]
                vb = ph.enter_context(sbt("vb", [128, TB // 128, D], BF16))
                vb_tok = GTok("vb")
                uT = ph.enter_context(sbt("uT", [128, KC, TB], BF16))
                uT_tok = [GTok("uT%d" % i) for i in range(NST)]
                wp_rot = Rot([(ph.enter_context(sbt("wp%d" % i, [128, H * 128], BF16)), GTok("wp%d" % i)) for i in range(2)])
                vf_rot = Rot([(ph.enter_context(sbt("vf%d" % i, [128, 512], F32)), GTok("vf%d" % i)) for i in range(2)])
                sspart = ph.enter_context(sbt("sspart", [128, TB // 128, 4], F32))
                ss8 = ph.enter_context(sbt("ss8", [128, TB // 128], F32))
                ss_tok = GTok("ss")
                extra_xt(ph, 3)
                wvi = wview(g_win[j])
                wvo = wview(g_wo[j])
                loads = []
                for blk in range(NB):
                    loads += panel_loads(wvi, KC, [pn * 512 for pn in range(8)])
                    loads += panel_loads(wvo, KC, [pn * 512 for pn in range(4)])
                ws = WStream(slots, loads)
                ws.get(0)
                NT = TB // 128
                first_norm(gsc1, sh1, vt)
                for blk in range(NB):
                    op(DVE, lambda: nc.vector.memset(sspart[:], 0.0), writes=[ss_tok])
                    for pn in range(8):
                        p3, ptok = ws.get(blk * 12 + pn)
                        if pn < 4:
                            for q in range(4):
                                g = pn * 4 + q
                                for st in range(NST):
                                    ps, pstok = next_ps()
                                    for kc in range(KC):
                                        op(PE, lambda: nc.tensor.matmul(ps[:], lhsT=p3[:, kc, q * 128:(q + 1) * 128],
                                                                        rhs=HT[:, kc, st * 512:(st + 1) * 512],
                                                                        start=(kc == 0), stop=(kc == KC - 1)),
                                           reads=[ptok, HT_tok[st]], writes=[pstok], signal=(kc == KC - 1))
                                    op(ACT, lambda: nc.scalar.activation(out=uT[:, g, st * 512:(st + 1) * 512], in_=ps[:], func=AF.Gelu_apprx_tanh),
                                       reads=[pstok], writes=[uT_tok[st]], dis=True)
                        else:
                            cc = pn - 4
                            for tt in range(NT):
                                st = tt // 4
                                ps, pstok = next_ps()
                                for kc in range(KC):
                                    op(PE, lambda: nc.tensor.matmul(ps[:], lhsT=HT[:, kc, tt * 128:(tt + 1) * 128], rhs=p3[:, kc, :],
                                                                    start=(kc == 0), stop=(kc == KC - 1)),
                                       reads=[ptok, HT_tok[st]], writes=[pstok], signal=(kc == KC - 1))
                                vf, vftok = vf_rot.next()
                                op(ACT, lambda: nc.scalar.activation(out=vf[:], in_=ps[:], func=AF.Gelu_apprx_tanh), reads=[pstok], writes=[vftok])
                                sq, sqtok = sq_rot.next()
                                op(ACT, lambda: nc.scalar.activation(out=sq[:], in_=vf[:], func=AF.Square, accum_out=sspart[:, tt, cc:cc + 1]),
                                   reads=[vftok], writes=[sqtok, ss_tok], dis=True)
                                op(DVE, lambda: nc.vector.tensor_tensor(out=vb[:, tt, cc * 512:(cc + 1) * 512], in0=vf[:], in1=vgb[:, cc * 512:(cc + 1) * 512], op=ALU.mult),
                                   reads=[vftok, vgb_tok], writes=[vb_tok], dis=True)
                                if pn == 7 and tt == 4:
                                    bg = norm_gen(xsrc, xsrc_toks, blk + 1, gsc1, sh1, vt, HT, HT_tok) if blk + 1 < NB else next_phase_norm(gsc2, sh2, vt)
                                    step(bg)
                    op(DVE, lambda: nc.vector.tensor_reduce(out=ss8[:], in_=sspart[:], axis=AX.X, op=ALU.add), reads=[ss_tok], writes=[ss_tok])
                    op(DVE, lambda: nc.vector.tensor_scalar(out=ss8[:], in0=ss8[:], scalar1=1.0 / D, scalar2=EPS, op0=ALU.mult, op1=ALU.add),
                       reads=[ss_tok], writes=[ss_tok])
                    op(DVE, lambda: nc.vector.reciprocal(out=ss8[:], in_=ss8[:]), reads=[ss_tok], writes=[ss_tok])
                    op(ACT, lambda: nc.scalar.activation(out=ss8[:], in_=ss8[:], func=AF.Sqrt), reads=[ss_tok], writes=[ss_tok])
                    for tt in range(NT):
                        st = tt // 4
                        wp, wptok = wp_rot.next()
                        op(DVE, lambda: nc.vector.tensor_scalar(out=wp[:], in0=wsT[:], scalar1=ss8[:, tt:tt + 1], scalar2=None, op0=ALU.mult),
                           reads=[wsT_tok, ss_tok], writes=[wptok])
                        for gq in range(4):
                            ps, pstok = next_ps()
                            for gi in range(4):
                                g = gq * 4 + gi
                                gs = slice(g * 128, (g + 1) * 128)
                                o_ = ps[:, gi * 128:(gi + 1) * 128]
                                op(PE, lambda: nc.tensor.matmul(o_, lhsT=vb[:, tt, gs], rhs=wp[:, gs], start=True, stop=False),
                                   reads=[vb_tok, wptok], writes=[pstok], signal=False)
                                op(PE, lambda: nc.tensor.matmul(o_, lhsT=ones[0:1, 0:128], rhs=bhi[0:1, gs], start=False, stop=False),
                                   reads=[bs_tok], writes=[pstok], signal=False)
                                op(PE, lambda: nc.tensor.matmul(o_, lhsT=ones[0:1, 0:128], rhs=blo[0:1, gs], start=False, stop=True),
                                   reads=[bs_tok], writes=[pstok], signal=(gi == 3))
                            uv = uT[:, gq * 4:(gq + 1) * 4, tt * 128:(tt + 1) * 128]
                            op(DVE, lambda: nc.vector.tensor_tensor(out=uv, in0=uv, in1=ps[:].rearrange("p (a b) -> p a b", a=4), op=ALU.mult),
                               reads=[pstok, uT_tok[st]], writes=[uT_tok[st]], dis=True)
                            if gq == 0:
                                step(bg)
                    out_proj(ws, blk * 12 + 8, blk, uT, uT_tok, KC, g1c, vt, xsrc, xsrc_toks, bg=bg, bg_n=2)
                    exhaust(bg)
                K.barrier()
                xrot_cur[0] = xt_rot
        xsrc, xsrc_toks = XTL
        if stop_after == ("mix", li):
            break

        with ExitStack() as ph:
            M = ph.enter_context(sbt("M", [128, FC, TB], BF16))
            M_tok = [[GTok("M%d_%d" % (i, g_)) for g_ in range((FC + 7) // 8)] for i in range(NST)]
            NWP = 3
            slots = [(ph.enter_context(sbt("fpan%d" % i, [128, 16 * 512], BF16)), GTok("pan%d" % i)) for i in range(NWP)]
            cw = ph.enter_context(sbt("cw", [128, 3 * 2 * FC], F32))
            cb = ph.enter_context(sbt("cb", [128, 2 * FC], F32))
            carry = ph.enter_context(sbt("carry", [128, 2 * FC, 2], F32))
            CV = GTok("convvec")
            carry_tok = GTok("carry")
            cf_rot = Rot([(ph.enter_context(sbt("cf%d" % i, [128, 512], F32)), GTok("cf%d" % i)) for i in range(4)])
            gs_rot = Rot([(ph.enter_context(sbt("gs%d" % i, [128, 512], F32)), GTok("gs%d" % i)) for i in range(2)])
            dma(SP, cw[:], f_cw[li], CV, writes=[CV])
            dma(SP, cb[:], f_cb[li], CV, writes=[CV], dis=True)
            op(DVE, lambda: nc.vector.memset(carry[:], 0.0), writes=[carry_tok])
            wvi = wview(f_win[li])
            wvo = wview(f_wout[li])
            steps = [(c, min(2, FC - c)) for c in range(0, FC, 2)]

            def mk_a(stp):
                def f(wp, wptok):
                    c0, nch = stp
                    w = nch * 128
                    g3 = wp[:, 0:16 * 256].rearrange("p (k n) -> p k n", k=16)
                    u3 = wp[:, 16 * 256:32 * 256].rearrange("p (k n) -> p k n", k=16)
                    dma(POOL, g3[:, :, 0:w], wvi[:, :, c0 * 128:c0 * 128 + w], wptok, writes=[wptok])
                    dma(POOL, u3[:, :, 0:w], wvi[:, :, DFF + c0 * 128:DFF + c0 * 128 + w], wptok, writes=[wptok], dis=True)
                    return g3, u3
                return f

            def mk_b(n):
                def f(wp, wptok):
                    p3 = wp[:, 0:FC * 128].rearrange("p (k n) -> p k n", k=FC)
                    dma(POOL, p3[:, 0:16, :], wvo[:, 0:16, n * 128:(n + 1) * 128], wptok, writes=[wptok])
                    dma(POOL, p3[:, 16:32, :], wvo[:, 16:32, n * 128:(n + 1) * 128], wptok, writes=[wptok], dis=True)
                    dma(POOL, p3[:, 32:FC, :], wvo[:, 32:FC, n * 128:(n + 1) * 128], wptok, writes=[wptok], dis=True)
                    return p3
                return f

            loads = []
            for blk in range(NB):
                loads += [mk_a(stp) for stp in steps]
                loads += [mk_b(n) for n in range(KC)]
            per_blk = len(steps) + KC
            ws = WStream(slots, loads)
            ws.get(0)

            def conv(ps, pstok, cidx):
                cf, cftok = cf_rot.next()
                w0 = cw[:, cidx:cidx + 1]
                w1 = cw[:, 2 * FC + cidx:2 * FC + cidx + 1]
                w2 = cw[:, 4 * FC + cidx:4 * FC + cidx + 1]
                op(ACT, lambda: nc.scalar.activation(out=cf[:], in_=ps[:], func=AF.Identity, scale=w2, bias=cb[:, cidx:cidx + 1]),
                   reads=[pstok, CV], writes=[cftok])
                op(DVE, lambda: nc.vector.scalar_tensor_tensor(out=cf[:, 1:512], in0=ps[:, 0:511], scalar=w1, in1=cf[:, 1:512], op0=ALU.mult, op1=ALU.add),
                   reads=[pstok, cftok, CV], writes=[cftok])
                op(DVE, lambda: nc.vector.scalar_tensor_tensor(out=cf[:, 2:512], in0=ps[:, 0:510], scalar=w0, in1=cf[:, 2:512], op0=ALU.mult, op1=ALU.add),
                   reads=[pstok, cftok, CV], writes=[cftok])
                op(DVE, lambda: nc.vector.scalar_tensor_tensor(out=cf[:, 0:2], in0=carry[:, cidx, 0:2], scalar=w0, in1=cf[:, 0:2], op0=ALU.mult, op1=ALU.add),
                   reads=[carry_tok, cftok, CV], writes=[cftok])
                op(DVE, lambda: nc.vector.scalar_tensor_tensor(out=cf[:, 0:1], in0=carry[:, cidx, 1:2], scalar=w1, in1=cf[:, 0:1], op0=ALU.mult, op1=ALU.add),
                   reads=[carry_tok, cftok, CV], writes=[cftok])
                op(ACT, lambda: nc.scalar.activation(out=carry[:, cidx, 0:2], in_=ps[:, 510:512], func=AF.Copy),
                   reads=[pstok, carry_tok], writes=[carry_tok])
                return cf, cftok

            first_norm(gsc2, sh2, vt)
            for blk in range(NB):
                bg = None
                for si, stp in enumerate(steps):
                    if si == len(steps) - 1:
                        if blk + 1 < NB:
                            bg = norm_gen(xsrc, xsrc_toks, blk + 1, gsc2, sh2, vt, HT, HT_tok)
                        elif li + 1 < n_layers and (li + 1) in mod_done and stop_after is None:
                            l2 = li + 1
                            bg = next_phase_norm(gsc1all[:, l2 * KC:(l2 + 1) * KC], modall[:, l2 * 96:l2 * 96 + 16], VECL[l2])
                        step(bg)
                    (g3, u3), wptok = ws.get(blk * per_blk + si)
                    c0, nch = stp
                    for ci in range(nch):
                        c = c0 + ci
                        for st in range(NST):
                            psg, psgtok = next_ps()
                            psu, psutok = next_ps()
                            for kc in range(KC):
                                op(PE, lambda: nc.tensor.matmul(psg[:], lhsT=g3[:, kc, ci * 128:(ci + 1) * 128], rhs=HT[:, kc, st * 512:(st + 1) * 512],
                                                                start=(kc == 0), stop=(kc == KC - 1)),
                                   reads=[wptok, HT_tok[st]], writes=[psgtok], signal=(kc == KC - 1))
                            for kc in range(KC):
                                op(PE, lambda: nc.tensor.matmul(psu[:], lhsT=u3[:, kc, ci * 128:(ci + 1) * 128], rhs=HT[:, kc, st * 512:(st + 1) * 512],
                                                                start=(kc == 0), stop=(kc == KC - 1)),
                                   reads=[wptok, HT_tok[st]], writes=[psutok], signal=(kc == KC - 1))
                            cg, cgtok = conv(psg, psgtok, c)
                            cu, cutok = conv(psu, psutok, FC + c)
                            gs_, gstok = gs_rot.next()
                            op(ACT, lambda: nc.scalar.activation(out=gs_[:], in_=cg[:], func=AF.Silu), reads=[cgtok], writes=[gstok])
                            op(DVE, lambda: nc.vector.tensor_tensor(out=M[:, c, st * 512:(st + 1) * 512], in0=gs_[:], in1=cu[:], op=ALU.mult),
                               reads=[gstok, cutok], writes=[M_tok[st][c // 8]], dis=True)
                groups = [(n, st) for n in range(KC) for st in range(NST)]
                nxt = resid_load(0, blk * NST, xsrc, xsrc_toks)
                for gi, (n, st) in enumerate(groups):
                    p3, wptok = ws.get(blk * per_blk + len(steps) + n)
                    cur = nxt
                    if gi + 1 < len(groups):
                        nxt = resid_load(groups[gi + 1][0], blk * NST + groups[gi + 1][1], xsrc, xsrc_toks)
                    ps, pstok = next_ps()
                    for kc in range(FC):
                        op(PE, lambda: nc.tensor.matmul(ps[:], lhsT=p3[:, kc, :], rhs=M[:, kc, st * 512:(st + 1) * 512],
                                                        start=(kc == 0), stop=(kc == FC - 1)),
                           reads=[wptok, M_tok[st][kc // 8]], writes=[pstok], signal=(kc == FC - 1))
                    step(bg, 2)
                    resid_apply(ps, pstok, cur, n, blk * NST + st, g2c[:, n:n + 1], vt)
                exhaust(bg)
            K.barrier()
        if stop_after == ("ffn", li):
            break

    if do_final:
        with ExitStack() as ph:
            frot = Rot(list(nx_rot.items) + [(ph.enter_context(sbt("nxf%d" % i, [128, 512], F32)), GTok("nxf%d" % i)) for i in range(8)])
            for blk in range(NB):
                exhaust(norm_gen(xsrc, xsrc_toks, blk, fing_sb, None, None, to_out=True, rot=frot, PF=6))
            K.barrier()
    else:
        for sg in range(S // 512):
            for c in range(KC):
                xt, xtok = xt_rot.next()
                dma(SP, xt[:], xsrc[c * 128:(c + 1) * 128, sg * 512:(sg + 1) * 512], xtok, reads=[xsrc_toks[sg]], writes=[xtok])
                dma(SP, outT[c * 128:(c + 1) * 128, sg * 512:(sg + 1) * 512], xt[:], xtok, reads=[xtok])
    K.barrier()
    K.es.close()
    return nc


def _consts():
    ident = np.eye(128, dtype=np.float32)
    p = np.arange(128)[:, None]
    cidx = np.arange(512)[None, :]
    masks = np.zeros((128, 4, 512), np.float32)
    for jj in range(4):
        masks[:, jj, :] = np.where(cidx - p >= 128 * jj, 0.0, -30000.0)
    t = np.arange(128)[None, :]
    tri = (t >= p).astype(np.float32)
    tri = np.tile(tri[:, None, :], (1, H, 1)).reshape(128, H * 128)
    return ident, masks.reshape(128, 4 * 512), np.ascontiguousarray(tri)


def prep_inputs(inputs):
    f = lambda a: np.ascontiguousarray(np.asarray(a, dtype=np.float32))
    x = f(inputs["x"])
    c = f(inputs["c"])
    ident, masks, tri = _consts()
    shared = {
        "mod_w": f(inputs["mod_w"]),
        "mod_b": f(np.asarray(inputs["mod_b"]).reshape(4, 96, 128).transpose(2, 0, 1).reshape(128, 4 * 96)),
        "mixg": f(np.asarray(inputs["mix_norm_g"]).reshape(4, KC, 128).transpose(2, 0, 1).reshape(128, 4 * KC)),
        "ffng": f(np.asarray(inputs["ffn_norm_g"]).reshape(4, KC, 128).transpose(2, 0, 1).reshape(128, 4 * KC)),
        "a_win": f(inputs["attn_w_in"]),
        "a_bf": f(np.asarray(inputs["attn_b_f"]).reshape(2, H, 1)),
        "a_wo": f(inputs["attn_w_o"]),
        "g_win": f(inputs["gm_w_in"]),
        "g_vg": f(np.asarray(inputs["gm_v_g"]).reshape(2, 1, D)),
        "g_ws": f(np.asarray(inputs["gm_w_s"]).transpose(0, 3, 1, 2).reshape(2, 128, H * 128)),
        "g_bs": f(np.asarray(inputs["gm_b_s"]).reshape(2, 1, D)),
        "g_wo": f(inputs["gm_w_o"]),
        "f_win": f(inputs["ffn_w_in"]),
        "f_cw": f(np.asarray(inputs["ffn_conv_w"]).reshape(4, 3, 2 * FC, 128).transpose(0, 3, 1, 2).reshape(4, 128, 3 * 2 * FC)),
        "f_cb": f(np.asarray(inputs["ffn_conv_b"]).reshape(4, 2 * FC, 128).transpose(0, 2, 1)),
        "f_wout": f(inputs["ffn_w_out"]),
        "fin_g": f(np.asarray(inputs["final_g"]).reshape(KC, 128).T),
        "c_ident": ident, "c_masks": masks, "c_tri": tri,
    }
    in_maps = []
    for b in range(NCORES):
        m = dict(shared)
        m["xT"] = np.ascontiguousarray(x[b].T)
        m["c_t"] = np.ascontiguousarray(c[b].reshape(KC, 128).T)
        in_maps.append(m)
    return in_maps


def kernel(**inputs):
    in_maps = prep_inputs(inputs)
    nc = build()
    res = run_bass_kernel_spmd(nc, in_maps, core_ids=list(range(NCORES)))
    out = np.stack([np.ascontiguousarray(np.asarray(r["outT"]).T) for r in res.results], axis=0)
    return out.astype(np.float32)
```

```python
import numpy as np
from contextlib import ExitStack
import concourse.bass as bass
import concourse.mybir as mybir
from concourse.bass_utils import run_bass_kernel_spmd

F32 = mybir.dt.float32
BF16 = mybir.dt.bfloat16
AF = mybir.ActivationFunctionType
ALU = mybir.AluOpType
AX = mybir.AxisListType

D = 2048
S = 4096
KC = 16
DFF = 5504
FC = 43
H = 16
DH = 128
NST = 2
TB = 512 * NST
NB = S // TB
EPS = 1e-6
NCORES = 8
SAME_ENGINE_SYNC = True


class Sem:
    def __init__(self, h, dma):
        self.h = h
        self.dma = dma
        self.issued = 0


class Tok:
    def __init__(self, name):
        self.name = name
        self.w = {}
        self.r = {}
        self.sem = None
        self.pend = None


class Eng:
    def __init__(self, q, sem, name, inorder_safe=False):
        self.q = q
        self.sem = sem
        self.name = name
        self.known = {}
        self.pending = []
        self.inorder_safe = inorder_safe


class Ctx:
    def __init__(self):
        self.nc = bass.Bass("TRN2", target_bir_lowering=False)
        self.es = ExitStack()
        self.sems = []
        self.nsem = 0
        nc = self.nc
        self.PE = Eng(nc.tensor, self.new_sem(False), "pe", inorder_safe=True)
        self.ACT = Eng(nc.scalar, self.new_sem(False), "act")
        self.DVE = Eng(nc.vector, self.new_sem(False), "dve")
        self.POOL = Eng(nc.gpsimd, self.new_sem(False), "pool")
        self.SP = Eng(nc.sync, None, "sp")
        self.engs = [self.PE, self.ACT, self.DVE, self.POOL, self.SP]

    def new_sem(self, dma):
        self.nsem += 1
        h = self.es.enter_context(self.nc.semaphore("s%d" % self.nsem))
        s = Sem(h, dma)
        self.sems.append(s)
        return s

    def _deps(self, eng, reads, writes, dis):
        ev = {}

        def add(d):
            for k, e in d.items():
                if k not in ev or ev[k][1] < e[1]:
                    ev[k] = e

        for t in reads:
            if t.pend is not None and t.pend is not eng:
                raise RuntimeError("token %s has pending unsignaled access" % t.name)
            add(t.w)
        for t in writes:
            if t.pend is not None and t.pend is not eng:
                raise RuntimeError("token %s has pending unsignaled access" % t.name)
            add(t.r)
            if not dis:
                add(t.w)
        return ev

    def _wait(self, eng, ev):
        for k, (sm, v) in ev.items():
            if sm is eng.sem and (eng.inorder_safe or not SAME_ENGINE_SYNC):
                continue
            val = sm.issued if sm.dma else v
            if eng.known.get(k, 0) >= val:
                continue
            eng.q.wait_ge(sm.h, val)
            eng.known[k] = val

    @staticmethod
    def _record(event, reads, writes, dis):
        k = id(event[0])
        for t in reads:
            t.r[k] = event
        for t in writes:
            if dis:
                t.w[k] = event
            else:
                t.w = {k: event}
                t.r = {}

    def op(self, eng, fn, reads=(), writes=(), signal=True, dis=False):
        ev = self._deps(eng, reads, writes, dis)
        self._wait(eng, ev)
        inst = fn()
        if signal:
            eng.sem.issued += 1
            inst.then_inc(eng.sem.h, 1)
            event = (eng.sem, eng.sem.issued)
            for (r, w, d) in eng.pending:
                self._record(event, r, w, d)
                for t in list(r) + list(w):
                    t.pend = None
            eng.pending = []
            self._record(event, reads, writes, dis)
        else:
            eng.pending.append((tuple(reads), tuple(writes), dis))
            for t in list(reads) + list(writes):
                t.pend = eng
        return inst

    def dma(self, eng, out, in_, sbtok, reads=(), writes=(), dis=False):
        ev = self._deps(eng, reads, writes, dis)
        self._wait(eng, ev)
        if sbtok.sem is None:
            sbtok.sem = self.new_sem(True)
        sm = sbtok.sem
        eng.q.dma_start(out=out, in_=in_).then_inc(sm.h, 16)
        sm.issued += 16
        self._record((sm, sm.issued), reads, writes, dis)

    def barrier(self):
        for e in self.engs:
            assert not e.pending, "pending ops at barrier on %s" % e.name
        for e in self.engs:
            for sm in self.sems:
                if sm.issued == 0:
                    continue
                k = id(sm)
                if e.known.get(k, 0) >= sm.issued:
                    continue
                if sm is e.sem and e.inorder_safe:
                    continue
                e.q.wait_ge(sm.h, sm.issued)
                e.known[k] = sm.issued


class Rot:
    def __init__(self, items):
        self.items = items
        self.i = 0

    def next(self):
        it = self.items[self.i % len(self.items)]
        self.i += 1
        return it


class WStream:
    def __init__(self, slots, loads):
        self.slots = slots
        self.loads = loads
        self.views = [None] * len(loads)
        self.issued = 0

    def get(self, i):
        ns = len(self.slots)
        lim = min(len(self.loads), i + ns)
        while self.issued < lim:
            j = self.issued
            ap, tok = self.slots[j % ns]
            self.views[j] = (self.loads[j](ap, tok), tok)
            self.issued += 1
        return self.views[i]


def step(gen, n=1):
    if gen is None:
        return
    for _ in range(n):
        try:
            next(gen)
        except StopIteration:
            return


def exhaust(gen):
    if gen is None:
        return
    for _ in gen:
        pass


def build(n_layers=4, do_final=True, stop_after=None):
    K = Ctx()
    nc = K.nc
    PE, ACT, DVE, POOL, SP = K.PE, K.ACT, K.DVE, K.POOL, K.SP
    op, dma = K.op, K.dma

    def din(name, shape, dt=F32):
        return nc.dram_tensor(name, list(shape), dt, kind="ExternalInput").ap()

    xT_in = din("xT", [D, S])
    c_t = din("c_t", [128, KC])
    mod_w = din("mod_w", [4, D, 6 * D])
    mod_b = din("mod_b", [128, 4 * 96])
    mixg = din("mixg", [128, 4 * KC])
    ffng = din("ffng", [128, 4 * KC])
    a_win = din("a_win", [2, D, 3 * D + H])
    a_bf = din("a_bf", [2, H, 1])
    a_wo = din("a_wo", [2, D, D])
    g_win = din("g_win", [2, D, 2 * D])
    g_vg = din("g_vg", [2, 1, D])
    g_ws = din("g_ws", [2, 128, H * 128])
    g_bs = din("g_bs", [2, 1, D])
    g_wo = din("g_wo", [2, D, D])
    f_win = din("f_win", [4, D, 2 * DFF])
    f_cw = din("f_cw", [4, 128, 3 * 2 * FC])
    f_cb = din("f_cb", [4, 128, 2 * FC])
    f_wout = din("f_wout", [4, DFF, D])
    fin_g = din("fin_g", [128, KC])
    c_ident = din("c_ident", [128, 128])
    c_masks = din("c_masks", [128, 4 * 512])
    c_tri = din("c_tri", [128, H * 128])
    outT = nc.dram_tensor("outT", [D, S], F32, kind="ExternalOutput").ap()

    XT = nc.dram_tensor("XT", [D, S], F32).ap()
    QT = nc.dram_tensor("QT", [D, S], BF16).ap()
    KT = nc.dram_tensor("KT", [D, S], BF16).ap()
    OT = nc.dram_tensor("OT", [D, S], BF16).ap()
    VV = nc.dram_tensor("VV", [S, D], BF16).ap()
    FR = nc.dram_tensor("FR", [6, H, S], BF16).ap()

    XT_tok = [Tok("XT%d" % i) for i in range(S // 512)]
    QT_tok, KT_tok, VV_tok, OT_tok, FR_tok = Tok("QT"), Tok("KT"), Tok("VV"), Tok("OT"), Tok("FR")
    NOTOK = Tok("ro")
    _gt = {}
    _uid = [0]

    def sbt(name, shape, dt):
        _uid[0] += 1
        return nc.sbuf_tensor("%s_u%d" % (name, _uid[0]), shape, dt)

    def GTok(name):
        if name not in _gt:
            _gt[name] = Tok(name)
        return _gt[name]

    es = K.es

    def sb(name, shape, dt):
        return es.enter_context(nc.sbuf_tensor(name, list(shape), dt))

    HT = sb("HT", [128, KC, TB], BF16)
    HT_tok = [Tok("HT%d" % i) for i in range(NST)]
    nx_rot = Rot([(sb("nx%d" % i, [128, 512], F32), Tok("nx%d" % i)) for i in range(4)])
    xt_rot = Rot([(sb("xt%d" % i, [128, 512], F32), Tok("xt%d" % i)) for i in range(3)])
    sq_rot = Rot([(sb("sq%d" % i, [128, 512], BF16), Tok("sq%d" % i)) for i in range(2)])
    rstd_rot = Rot([(sb("rstd%d" % i, [128, 512], F32), Tok("rstd%d" % i)) for i in range(1)])
    modall = sb("modall", [128, 4 * 96], F32)
    modb_sb = sb("modb_sb", [128, 4 * 96], F32)
    gsc1all = sb("gsc1all", [128, 4 * KC], F32)
    gsc2all = sb("gsc2all", [128, 4 * KC], F32)
    mixg_sb = sb("mixg_sb", [128, 4 * KC], F32)
    ffng_sb = sb("ffng_sb", [128, 4 * KC], F32)
    fing_sb = sb("fing_sb", [128, KC], F32)
    cin = sb("cin", [128, KC], F32)
    cact = sb("cact", [128, KC], BF16)
    ident = sb("ident", [128, 128], BF16)
    ones = sb("ones", [128, 128], BF16)
    VECL = [Tok("vec%d" % i) for i in range(4)]
    CONST = Tok("const")
    PS = [es.enter_context(nc.psum_tensor("ps%d" % i, [128, 512], F32)) for i in range(8)]
    PS_tok = [Tok("ps%d" % i) for i in range(8)]
    ps_rr = [0]

    def next_ps():
        i = ps_rr[0] % 7
        ps_rr[0] += 1
        return PS[i], PS_tok[i]

    dma(POOL, ident[:], c_ident, CONST, writes=[CONST], dis=True)
    op(DVE, lambda: nc.vector.memset(ones[:], 1.0), writes=[CONST], dis=True)
    dma(SP, cin[:], c_t, CONST, writes=[CONST], dis=True)
    dma(SP, fing_sb[:], fin_g, CONST, writes=[CONST], dis=True)
    dma(SP, modb_sb[:], mod_b, CONST, writes=[CONST], dis=True)
    dma(SP, mixg_sb[:], mixg, CONST, writes=[CONST], dis=True)
    dma(SP, ffng_sb[:], ffng, CONST, writes=[CONST], dis=True)
    op(ACT, lambda: nc.scalar.activation(out=cact[:], in_=cin[:], func=AF.Silu), reads=[CONST], writes=[CONST], dis=True)
    K.barrier()

    def wview(w2d):
        return w2d.rearrange("(kc p) n -> p kc n", p=128)

    def load_panel(dst3, tok, wv, kc0, kc1, n0, n1):
        k = kc0
        while k < kc1:
            ke = min(k + 16, kc1)
            dma(POOL, dst3[:, k - kc0:ke - kc0, 0:n1 - n0], wv[:, k:ke, n0:n1], tok, writes=[tok], dis=(k > kc0))
            k = ke
        return dst3

    def panel_loads(wv, nk, cols):
        return [(lambda ap, tok, n0=n0: load_panel(ap, tok, wv, 0, nk, n0, n0 + 512)) for n0 in cols]

    PF = 2

    def norm_gen(xsrc, xsrc_toks, blk, gcol, bcol, vtok, HTd=None, HTd_tok=None, to_out=False, rot=None, PF=2):
        pss, pss_tok = PS[7], PS_tok[7]
        seq = [(st, p, c) for st in range(NST) for p in (1, 2) for c in range(KC)]
        tiles = {}

        def ld(k):
            st, p, c = seq[k]
            t0 = (blk * NST + st) * 512
            xt, xtok = (rot or nx_rot).next()
            dma(SP, xt[:], xsrc[c * 128:(c + 1) * 128, t0:t0 + 512], xtok, reads=[xsrc_toks[blk * NST + st]], writes=[xtok])
            tiles[k] = (xt, xtok)

        for k in range(min(PF, len(seq))):
            ld(k)
        yield
        rs = rstok = None
        for k, (st, p, c) in enumerate(seq):
            if k + PF < len(seq):
                ld(k + PF)
            t0 = (blk * NST + st) * 512
            xt, xtok = tiles.pop(k)
            if p == 1:
                sq, sqtok = sq_rot.next()
                op(ACT, lambda: nc.scalar.activation(out=sq[:], in_=xt[:], func=AF.Square), reads=[xtok], writes=[sqtok])
                op(PE, lambda: nc.tensor.matmul(pss[:], lhsT=ones[:, 0:128], rhs=sq[:], start=(c == 0), stop=(c == KC - 1)),
                   reads=[sqtok], writes=[pss_tok], signal=True)
                if c == KC - 1:
                    rs, rstok = rstd_rot.next()
                    op(DVE, lambda: nc.vector.tensor_scalar(out=rs[:], in0=pss[:], scalar1=1.0 / D, scalar2=EPS, op0=ALU.mult, op1=ALU.add),
                       reads=[pss_tok], writes=[rstok])
                    op(DVE, lambda: nc.vector.reciprocal(out=rs[:], in_=rs[:]), reads=[rstok], writes=[rstok])
                    op(ACT, lambda: nc.scalar.activation(out=rs[:], in_=rs[:], func=AF.Sqrt), reads=[rstok], writes=[rstok])
            else:
                op(DVE, lambda: nc.vector.tensor_tensor(out=xt[:], in0=xt[:], in1=rs[:], op=ALU.mult), reads=[xtok, rstok], writes=[xtok])
                if to_out:
                    op(ACT, lambda: nc.scalar.activation(out=xt[:], in_=xt[:], func=AF.Identity, scale=gcol[:, c:c + 1]),
                       reads=[xtok], writes=[xtok])
                    dma(SP, outT[c * 128:(c + 1) * 128, t0:t0 + 512], xt[:], xtok, reads=[xtok])
                else:
                    op(ACT, lambda: nc.scalar.activation(out=HTd[:, c, st * 512:(st + 1) * 512], in_=xt[:], func=AF.Identity,
                                                         scale=gcol[:, c:c + 1], bias=bcol[:, c:c + 1]),
                       reads=[xtok, vtok], writes=[HTd_tok[st]], dis=True)
            yield

    xrot_cur = [xt_rot]

    def extra_xt(ph, n):
        items = list(xt_rot.items) + [(ph.enter_context(sbt("xtx%d" % i, [128, 512], F32)), GTok("xtx%d" % i)) for i in range(n)]
        xrot_cur[0] = Rot(items)

    def resid_load(n, sg, xsrc, xsrc_toks):
        t0 = sg * 512
        xt, xtok = xrot_cur[0].next()
        dma(SP, xt[:], xsrc[n * 128:(n + 1) * 128, t0:t0 + 512], xtok, reads=[xsrc_toks[sg]], writes=[xtok])
        return xt, xtok

    def resid_apply(ps, pstok, xtile, n, sg, gcolumn, vtok):
        t0 = sg * 512
        xt, xtok = xtile
        op(DVE, lambda: nc.vector.scalar_tensor_tensor(out=xt[:], in0=ps[:], scalar=gcolumn, in1=xt[:], op0=ALU.mult, op1=ALU.add),
           reads=[pstok, xtok, vtok], writes=[xtok])
        dma(SP, XT[n * 128:(n + 1) * 128, t0:t0 + 512], xt[:], xtok, reads=[xtok], writes=[XT_tok[sg]], dis=True)

    def out_proj(ws, wbase, blk, rhs3, rhs_toks, nk, gcols, vtok, xsrc, xsrc_toks, bg=None, bg_n=1):
        groups = [(pn, q, st) for pn in range(D // 512) for q in range(4) for st in range(NST)]
        nxt = resid_load(groups[0][0] * 4 + groups[0][1], blk * NST + groups[0][2], xsrc, xsrc_toks)
        for gi, (pn, q, st) in enumerate(groups):
            p3, ptok = ws.get(wbase + pn)
            n = pn * 4 + q
            cur = nxt
            if gi + 1 < len(groups):
                pn2, q2, st2 = groups[gi + 1]
                nxt = resid_load(pn2 * 4 + q2, blk * NST + st2, xsrc, xsrc_toks)
            ps, pstok = next_ps()
            for kc in range(nk):
                op(PE, lambda: nc.tensor.matmul(ps[:], lhsT=p3[:, kc, q * 128:(q + 1) * 128],
                                                rhs=rhs3[:, kc, st * 512:(st + 1) * 512],
                                                start=(kc == 0), stop=(kc == nk - 1)),
                   reads=[ptok, rhs_toks[st]], writes=[pstok], signal=(kc == nk - 1))
            step(bg, bg_n)
            resid_apply(ps, pstok, cur, n, blk * NST + st, gcols[:, n:n + 1], vtok)

    def mod_gen(li, slots):
        wv = wview(mod_w[li])
        psm, psm_tok = PS[7], PS_tok[7]
        ws = WStream(slots, panel_loads(wv, KC, [pn * 512 for pn in range(6 * D // 512)]))
        for pn in range(6 * D // 512):
            p3, ptok = ws.get(pn)
            for q in range(4):
                n = pn * 4 + q
                for kc in range(KC):
                    op(PE, lambda: nc.tensor.matmul(psm[:, n:n + 1], lhsT=p3[:, kc, q * 128:(q + 1) * 128], rhs=cact[:, kc:kc + 1],
                                                    start=(kc == 0), stop=(kc == KC - 1)),
                       reads=[ptok], writes=[psm_tok], signal=(kc == KC - 1))
                yield
        m0 = li * 96
        vt = VECL[li]
        op(DVE, lambda: nc.vector.tensor_tensor(out=modall[:, m0:m0 + 96], in0=psm[:, 0:96], in1=modb_sb[:, m0:m0 + 96], op=ALU.add),
           reads=[psm_tok], writes=[vt])
        op(DVE, lambda: nc.vector.scalar_tensor_tensor(out=gsc1all[:, li * KC:(li + 1) * KC], in0=modall[:, m0 + 16:m0 + 32], scalar=1.0,
                                                       in1=mixg_sb[:, li * KC:(li + 1) * KC], op0=ALU.add, op1=ALU.mult),
           reads=[vt], writes=[vt])
        op(DVE, lambda: nc.vector.scalar_tensor_tensor(out=gsc2all[:, li * KC:(li + 1) * KC], in0=modall[:, m0 + 64:m0 + 80], scalar=1.0,
                                                       in1=ffng_sb[:, li * KC:(li + 1) * KC], op0=ALU.add, op1=ALU.mult),
           reads=[vt], writes=[vt])
        yield

    xsrc, xsrc_toks = xT_in, [NOTOK] * (S // 512)
    XTL = [XT, XT_tok]
    mod_done = set()
    prenorm = [False]

    def first_norm(gcol, bcol, vtok_):
        if prenorm[0]:
            prenorm[0] = False
            return
        exhaust(norm_gen(xsrc, xsrc_toks, 0, gcol, bcol, vtok_, HT, HT_tok))

    def next_phase_norm(gcol, bcol, vtok_):
        prenorm[0] = True
        return norm_gen(XT, XT_tok, 0, gcol, bcol, vtok_, HT, HT_tok)

    def chain(gens):
        for g in gens:
            yield from g

    for li in range(n_layers):
        j = li // 2
        if li not in mod_done:
            with ExitStack() as ph:
                slots = [(ph.enter_context(sbt("mpan%d" % i, [128, KC, 512], BF16)), GTok("mpan%d" % i)) for i in range(3)]
                exhaust(mod_gen(li, slots))
                mod_done.add(li)
                K.barrier()
        vt = VECL[li]
        m0 = li * 96
        sh1 = modall[:, m0 + 0:m0 + 16]
        g1c = modall[:, m0 + 32:m0 + 48]
        sh2 = modall[:, m0 + 48:m0 + 64]
        g2c = modall[:, m0 + 80:m0 + 96]
        gsc1 = gsc1all[:, li * KC:(li + 1) * KC]
        gsc2 = gsc2all[:, li * KC:(li + 1) * KC]

        if li % 2 == 0:
            with ExitStack() as ph:
                fpan = ph.enter_context(sbt("fpan", [128, KC, H], BF16))
                fpan_tok = GTok("fpan")
                FN = ph.enter_context(sbt("FN", [H, S], F32))
                FN_tok = GTok("FN")
                ftmp = ph.enter_context(sbt("ftmp", [H, 512], F32))
                ftmp_tok = GTok("ftmp")
                nbf = ph.enter_context(sbt("nbf", [H, 1], F32))
                nbf_tok = GTok("nbf")
                pha = ExitStack()
                slots = [(pha.enter_context(sbt("qpan%d" % i, [128, KC, 512], BF16)), GTok("pan%d" % i)) for i in range(3)]
                HT2 = pha.enter_context(sbt("HT2", [128, KC, TB], BF16))
                HT2_tok = [GTok("HT2_%d" % i) for i in range(NST)]
                HTs = [(HT, HT_tok), (HT2, HT2_tok)]
                stg_rot = Rot([(pha.enter_context(sbt("stg%d" % i, [128, 512], BF16)), GTok("stg%d" % i)) for i in range(4)])
                wv = wview(a_win[j])
                ws = WStream(slots, panel_loads(wv, KC, [pn * 512 for pn in range(12)] * NB))
                ws.get(0)
                dma(POOL, fpan[:], wv[:, :, 3 * D:3 * D + H], fpan_tok, writes=[fpan_tok])
                dma(SP, nbf[:], a_bf[j], nbf_tok, writes=[nbf_tok])
                op(DVE, lambda: nc.vector.tensor_scalar(out=nbf[:], in0=nbf[:], scalar1=-1.0, scalar2=None, op0=ALU.mult),
                   reads=[nbf_tok], writes=[nbf_tok])
                first_norm(gsc1, sh1, vt)
                for blk in range(NB):
                    Hc, Hc_tok = HTs[blk % 2]
                    bg = None
                    if blk + 1 < NB:
                        Hn, Hn_tok = HTs[(blk + 1) % 2]
                        bg = norm_gen(xsrc, xsrc_toks, blk + 1, gsc1, sh1, vt, Hn, Hn_tok)
                    for pn in range(12):
                        p3, ptok = ws.get(blk * 12 + pn)
                        if pn < 8:
                            for q in range(4):
                                n = (pn % 4) * 4 + q
                                for st in range(NST):
                                    ps, pstok = next_ps()
                                    for kc in range(KC):
                                        op(PE, lambda: nc.tensor.matmul(ps[:], lhsT=p3[:, kc, q * 128:(q + 1) * 128],
                                                                        rhs=Hc[:, kc, st * 512:(st + 1) * 512],
                                                                        start=(kc == 0), stop=(kc == KC - 1)),
                                           reads=[ptok, Hc_tok[st]], writes=[pstok], signal=(kc == KC - 1))
                                    sg_, sgtok = stg_rot.next()
                                    t0 = (blk * NST + st) * 512
                                    if pn < 4:
                                        op(ACT, lambda: nc.scalar.activation(out=sg_[:], in_=ps[:], func=AF.Copy, scale=float(DH ** -0.5)),
                                           reads=[pstok], writes=[sgtok])
                                        dma(SP, QT[n * 128:(n + 1) * 128, t0:t0 + 512], sg_[:], sgtok, reads=[sgtok], writes=[QT_tok], dis=True)
                                    else:
                                        op(DVE, lambda: nc.vector.tensor_copy(out=sg_[:], in_=ps[:]), reads=[pstok], writes=[sgtok])
                                        dma(SP, KT[n * 128:(n + 1) * 128, t0:t0 + 512], sg_[:], sgtok, reads=[sgtok], writes=[KT_tok], dis=True)
                                    step(bg)
                        else:
                            cc = pn - 8
                            for tt in range(TB // 128):
                                st = tt // 4
                                ps, pstok = next_ps()
                                for kc in range(KC):
                                    op(PE, lambda: nc.tensor.matmul(ps[:], lhsT=Hc[:, kc, tt * 128:(tt + 1) * 128], rhs=p3[:, kc, :],
                                                                    start=(kc == 0), stop=(kc == KC - 1)),
                                       reads=[ptok, Hc_tok[st]], writes=[pstok], signal=(kc == KC - 1))
                                sg_, sgtok = stg_rot.next()
                                tk0 = blk * TB + tt * 128
                                if tt % 2 == 0:
                                    op(ACT, lambda: nc.scalar.activation(out=sg_[:], in_=ps[:], func=AF.Copy), reads=[pstok], writes=[sgtok])
                                else:
                                    op(DVE, lambda: nc.vector.tensor_copy(out=sg_[:], in_=ps[:]), reads=[pstok], writes=[sgtok])
                                dma(SP, VV[tk0:tk0 + 128, cc * 512:(cc + 1) * 512], sg_[:], sgtok, reads=[sgtok], writes=[VV_tok], dis=True)
                                step(bg)
                    for st in range(NST):
                        ps, pstok = next_ps()
                        for kc in range(KC):
                            op(PE, lambda: nc.tensor.matmul(ps[0:H, :], lhsT=fpan[:, kc, :], rhs=Hc[:, kc, st * 512:(st + 1) * 512],
                                                            start=(kc == 0), stop=(kc == KC - 1)),
                               reads=[fpan_tok, Hc_tok[st]], writes=[pstok], signal=(kc == KC - 1))
                        t0 = (blk * NST + st) * 512
                        op(ACT, lambda: nc.scalar.activation(out=ftmp[:], in_=ps[0:H, :], func=AF.Exp, scale=-1.0, bias=nbf[:, 0:1]),
                           reads=[pstok, nbf_tok], writes=[ftmp_tok])
                        op(ACT, lambda: nc.scalar.activation(out=FN[:, t0:t0 + 512], in_=ftmp[:], func=AF.Ln, bias=1.0),
                           reads=[ftmp_tok], writes=[FN_tok], dis=True)
                    exhaust(bg)
                K.barrier()
                pha.close()
                G = ph.enter_context(sbt("G", [H, S], F32))
                R1 = ph.enter_context(sbt("R1", [H, S], F32))
                spl = [ph.enter_context(sbt("spl%d" % i, [H, S], BF16)) for i in range(6)]
                GT = GTok("G")
                op(DVE, lambda: nc.vector.tensor_tensor_scan(out=G[:], data0=FN[:], data1=FN[:], initial=0.0, op0=ALU.add, op1=ALU.bypass),
                   reads=[FN_tok], writes=[GT])
                op(DVE, lambda: nc.vector.tensor_copy(out=spl[0][:], in_=G[:]), reads=[GT], writes=[GT], dis=True)
                op(DVE, lambda: nc.vector.tensor_tensor(out=R1[:], in0=G[:], in1=spl[0][:], op=ALU.subtract), reads=[GT], writes=[GT], dis=True)
                op(DVE, lambda: nc.vector.tensor_copy(out=spl[1][:], in_=R1[:]), reads=[GT], writes=[GT], dis=True)
                op(DVE, lambda: nc.vector.tensor_tensor(out=G[:], in0=R1[:], in1=spl[1][:], op=ALU.subtract), reads=[GT], writes=[GT], dis=True)
                op(DVE, lambda: nc.vector.tensor_copy(out=spl[2][:], in_=G[:]), reads=[GT], writes=[GT], dis=True)
                for i in range(3):
                    op(DVE, lambda: nc.vector.tensor_scalar(out=spl[3 + i][:], in0=spl[i][:], scalar1=-1.0, scalar2=None, op0=ALU.mult),
                       reads=[GT], writes=[GT], dis=True)
                for i in range(6):
                    dma(SP, FR[i], spl[i][:], GT, reads=[GT], writes=[FR_tok], dis=True)
                K.barrier()

            with ExitStack() as ph:
                sets = []
                for i in range(2):
                    d = dict(
                        q=ph.enter_context(sbt("aq%d" % i, [128, S], BF16)),
                        k=ph.enter_context(sbt("ak%d" % i, [128, S], BF16)),
                        v=ph.enter_context(sbt("av%d" % i, [128, S // 128, DH], BF16)),
                        rk=ph.enter_context(sbt("ark%d" % i, [6, S], BF16)),
                        rq=ph.enter_context(sbt("arq%d" % i, [6, S], BF16)),
                        tok=GTok("aset%d" % i))
                    sets.append(d)
                masks = ph.enter_context(sbt("masks", [128, 4, 512], BF16))
                mtok = GTok("masks")
                dma(POOL, masks[:], c_masks.rearrange("p (a b) -> p a b", a=4), mtok, writes=[mtok])
                pt_rot = Rot([(ph.enter_context(sbt("pt%d" % i, [128, 512], BF16)), GTok("pt%d" % i)) for i in range(4)])
                rl_rot = Rot([(ph.enter_context(sbt("rl%d" % i, [128, 512], F32)), GTok("rl%d" % i)) for i in range(2)])
                stg_rot = Rot([(ph.enter_context(sbt("stg%d" % i, [128, 512], BF16)), GTok("stg%d" % i)) for i in range(4)])
                mslots = [(ph.enter_context(sbt("mpan%d" % i, [128, KC, 512], BF16)), GTok("mpan%d" % i)) for i in range(3)]
                todo = [l for l in (li + 1, li + 2) if l < n_layers and l not in mod_done]
                bg = chain([mod_gen(l, mslots) for l in todo]) if todo else None
                bg_n = 2 if len(todo) > 1 else 1
                for l in todo:
                    mod_done.add(l)
                for d in sets:
                    op(DVE, lambda: nc.vector.memset(d["rk"][:], 1.0), writes=[d["tok"]], dis=True)
                    op(DVE, lambda: nc.vector.memset(d["rq"][:], 1.0), writes=[d["tok"]], dis=True)
                vview = VV.rearrange("(kt p) d -> p kt d", p=128)

                def load_head(h, d):
                    tk = d["tok"]
                    dma(SP, d["q"][:], QT[h * 128:(h + 1) * 128, :], tk, reads=[QT_tok], writes=[tk], dis=False)
                    dma(SP, d["k"][:], KT[h * 128:(h + 1) * 128, :], tk, reads=[KT_tok], writes=[tk], dis=True)
                    dma(SP, d["v"][:, 0:16, :], vview[:, 0:16, h * 128:(h + 1) * 128], tk, reads=[VV_tok], writes=[tk], dis=True)
                    dma(SP, d["v"][:, 16:32, :], vview[:, 16:32, h * 128:(h + 1) * 128], tk, reads=[VV_tok], writes=[tk], dis=True)
                    dma(SP, d["rk"][0:3, :], FR[0:3, h, :], tk, reads=[FR_tok], writes=[tk], dis=True)
                    dma(SP, d["rq"][3:6, :], FR[3:6, h, :], tk, reads=[FR_tok], writes=[tk], dis=True)

                load_head(0, sets[0])
                SB_ = [0, 1, 2]
                for h in range(H):
                    d = sets[h % 2]
                    if h + 1 < H:
                        load_head(h + 1, sets[(h + 1) % 2])
                    tk = d["tok"]
                    for Q in range(S // 512):
                        nk = 4 * Q + 4
                        po, potok = PS[3 + Q % 2], PS_tok[3 + Q % 2]
                        pl, pltok = PS[5 + Q % 2], PS_tok[5 + Q % 2]
                        qs = slice(Q * 512, (Q + 1) * 512)

                        def cols(jk):
                            return 128 * (jk - 4 * Q) if jk >= 4 * Q else 0

                        def qk(jk):
                            ps, pstok = PS[SB_[jk % 3]], PS_tok[SB_[jk % 3]]
                            ks = slice(jk * 128, (jk + 1) * 128)
                            diag = jk >= 4 * Q
                            c0 = cols(jk)
                            qsl = slice(Q * 512 + c0, (Q + 1) * 512)
                            op(PE, lambda: nc.tensor.matmul(ps[:, c0:512], lhsT=d["k"][:, ks], rhs=d["q"][:, qsl], start=True, stop=False),
                               reads=[tk], writes=[pstok], signal=False)
                            if diag:
                                op(PE, lambda: nc.tensor.matmul(ps[:, c0:c0 + 128], lhsT=ident[:, :], rhs=masks[:, 0, 0:128], start=False, stop=False),
                                   reads=[mtok], writes=[pstok], signal=False)
                            op(PE, lambda: nc.tensor.matmul(ps[:, c0:512], lhsT=d["rk"][0:6, ks], rhs=d["rq"][0:6, qsl], start=False, stop=True),
                               reads=[tk], writes=[pstok], signal=True)

                        qk(0)
                        for jk in range(nk):
                            if jk + 1 < nk:
                                qk(jk + 1)
                            ps, pstok = PS[SB_[jk % 3]], PS_tok[SB_[jk % 3]]
                            c0 = cols(jk)
                            pt, pttok = pt_rot.next()
                            op(ACT, lambda: nc.scalar.activation(out=pt[:, c0:512], in_=ps[:, c0:512], func=AF.Exp), reads=[pstok], writes=[pttok])
                            op(PE, lambda: nc.tensor.matmul(po[:, c0:512], lhsT=d["v"][:, jk, :], rhs=pt[:, c0:512], start=(jk == 0), stop=(jk == nk - 1)),
                               reads=[tk, pttok], writes=[potok], signal=False)
                            op(PE, lambda: nc.tensor.matmul(pl[:, c0:512], lhsT=ones[:, :], rhs=pt[:, c0:512], start=(jk == 0), stop=(jk == nk - 1)),
                               reads=[pttok], writes=[pltok], signal=True)
                        rl, rltok = rl_rot.next()
                        op(DVE, lambda: nc.vector.reciprocal(out=rl[:], in_=pl[:]), reads=[pltok], writes=[rltok])
                        sg_, sgtok = stg_rot.next()
                        op(DVE, lambda: nc.vector.tensor_tensor(out=sg_[:], in0=po[:], in1=rl[:], op=ALU.mult),
                           reads=[potok, rltok], writes=[sgtok])
                        dma(SP, OT[h * 128:(h + 1) * 128, qs], sg_[:], sgtok, reads=[sgtok], writes=[OT_tok], dis=True)
                        step(bg, bg_n)
                exhaust(bg)
                K.barrier()

            with ExitStack() as ph:
                slots = [(ph.enter_context(sbt("opan%d" % i, [128, KC, 512], BF16)), GTok("opan%d" % i)) for i in range(4)]
                HT2 = ph.enter_context(sbt("HT2", [128, KC, TB], BF16))
                HT2_tok = [GTok("HT2_%d" % i) for i in range(NST)]
                HTs = [(HT, HT_tok), (HT2, HT2_tok)]
                extra_xt(ph, 5)
                otv = OT.rearrange("(kc p) t -> p kc t", p=128)

                class _Res:
                    def __init__(self):
                        wvo_ = wview(a_wo[j])
                        self.v = [(load_panel(ap, tok, wvo_, 0, KC, pn * 512, (pn + 1) * 512), tok) for pn, (ap, tok) in enumerate(slots)]

                    def get(self, i):
                        return self.v[i % 4]

                ws = _Res()

                def load_o(blk):
                    Hd, Hd_tok = HTs[blk % 2]
                    for st in range(NST):
                        t0 = (blk * NST + st) * 512
                        dma(SP, Hd[:, :, st * 512:(st + 1) * 512], otv[:, :, t0:t0 + 512], Hd_tok[st], reads=[OT_tok], writes=[Hd_tok[st]])

                load_o(0)
                for blk in range(NB):
                    if blk + 1 < NB:
                        load_o(blk + 1)
                    Hc, Hc_tok = HTs[blk % 2]
                    bgn = None
                    if blk == NB - 1 and Hc is not HT:
                        bgn = next_phase_norm(gsc2, sh2, vt)
                    out_proj(ws, blk * 4, blk, Hc, Hc_tok, KC, g1c, vt, xsrc, xsrc_toks, bg=bgn, bg_n=2)
                    exhaust(bgn)
                K.barrier()
                xrot_cur[0] = xt_rot
        else:
            with ExitStack() as ph:
                wsT = ph.enter_context(sbt("wsT", [128, H * 128], BF16))
                wsT_tok = GTok("wsT")
                vgb = ph.enter_context(sbt("vgb", [128, D], F32))
                vgb_tok = GTok("vgb")
                bhi = ph.enter_context(sbt("bhi", [1, D], BF16))
                blo = ph.enter_context(sbt("blo", [1, D], BF16))
                bs_tok = GTok("bs")
                with ExitStack() as ph2:
                    bsf = ph2.enter_context(sbt("bsf", [1, D], F32))
                    wsf = ph2.enter_context(sbt("wsf", [128, H * 128], F32))
                    tri = ph2.enter_context(sbt("tri", [128, H * 128], F32))
                    dma(SP, wsf[:], g_ws[j], wsT_tok, writes=[wsT_tok])
                    dma(SP, tri[:], c_tri, wsT_tok, writes=[wsT_tok], dis=True)
                    op(DVE, lambda: nc.vector.tensor_tensor(out=wsT[:], in0=wsf[:], in1=tri[:], op=ALU.mult), reads=[wsT_tok], writes=[wsT_tok], dis=True)
                    dma(SP, vgb[:], g_vg[j].partition_broadcast(128), vgb_tok, writes=[vgb_tok])
                    dma(SP, bsf[:], g_bs[j], bs_tok, writes=[bs_tok])
                    op(DVE, lambda: nc.vector.tensor_copy(out=bhi[:], in_=bsf[:]), reads=[bs_tok], writes=[bs_tok], dis=True)
                    op(DVE, lambda: nc.vector.tensor_tensor(out=bsf[:], in0=bsf[:], in1=bhi[:], op=ALU.subtract), reads=[bs_tok], writes=[bs_tok], dis=True)
                    op(DVE, lambda: nc.vector.tensor_copy(out=blo[:], in_=bsf[:]), reads=[bs_tok], writes=[bs_tok], dis=True)
                    K.barrier()
                slots = [(ph.enter_context(sbt("gpan%d" % i, [128, KC, 512], BF16)), GTok("pan%d" % i)) for i in range(3)]
                vb = ph.enter_context(sbt("vb", [128, TB // 128, D], BF16))
                vb_tok = GTok("vb")
                uT = ph.enter_context(sbt("uT", [128, KC, TB], BF16))
                uT_tok = [GTok("uT%d" % i) for i in range(NST)]
                wp_rot = Rot([(ph.enter_context(sbt("wp%d" % i, [128, H * 128], BF16)), GTok("wp%d" % i)) for i in range(2)])
                vf_rot = Rot([(ph.enter_context(sbt("vf%d" % i, [128, 512], F32)), GTok("vf%d" % i)) for i in range(2)])
                sspart = ph.enter_context(sbt("sspart", [128, TB // 128, 4], F32))
                ss8 = ph.enter_context(sbt("ss8", [128, TB // 128], F32))
                ss_tok = GTok("ss")
                extra_xt(ph, 3)
                wvi = wview(g_win[j])
                wvo = wview(g_wo[j])
                loads = []
                for blk in range(NB):
                    loads += panel_loads(wvi, KC, [pn * 512 for pn in range(8)])
                    loads += panel_loads(wvo, KC, [pn * 512 for pn in range(4)])
                ws = WStream(slots, loads)
                ws.get(0)
                NT = TB // 128
                first_norm(gsc1, sh1, vt)
                for blk in range(NB):
                    op(DVE, lambda: nc.vector.memset(sspart[:], 0.0), writes=[ss_tok])
                    for pn in range(8):
                        p3, ptok = ws.get(blk * 12 + pn)
                        if pn < 4:
                            for q in range(4):
                                g = pn * 4 + q
                                for st in range(NST):
                                    ps, pstok = next_ps()
                                    for kc in range(KC):
                                        op(PE, lambda: nc.tensor.matmul(ps[:], lhsT=p3[:, kc, q * 128:(q + 1) * 128],
                                                                        rhs=HT[:, kc, st * 512:(st + 1) * 512],
                                                                        start=(kc == 0), stop=(kc == KC - 1)),
                                           reads=[ptok, HT_tok[st]], writes=[pstok], signal=(kc == KC - 1))
                                    op(ACT, lambda: nc.scalar.activation(out=uT[:, g, st * 512:(st + 1) * 512], in_=ps[:], func=AF.Gelu_apprx_tanh),
                                       reads=[pstok], writes=[uT_tok[st]], dis=True)
                        else:
                            cc = pn - 4
                            for tt in range(NT):
                                st = tt // 4
                                ps, pstok = next_ps()
                                for kc in range(KC):
                                    op(PE, lambda: nc.tensor.matmul(ps[:], lhsT=HT[:, kc, tt * 128:(tt + 1) * 128], rhs=p3[:, kc, :],
                                                                    start=(kc == 0), stop=(kc == KC - 1)),
                                       reads=[ptok, HT_tok[st]], writes=[pstok], signal=(kc == KC - 1))
                                vf, vftok = vf_rot.next()
                                op(ACT, lambda: nc.scalar.activation(out=vf[:], in_=ps[:], func=AF.Gelu_apprx_tanh), reads=[pstok], writes=[vftok])
                                sq, sqtok = sq_rot.next()
                                op(ACT, lambda: nc.scalar.activation(out=sq[:], in_=vf[:], func=AF.Square, accum_out=sspart[:, tt, cc:cc + 1]),
                                   reads=[vftok], writes=[sqtok, ss_tok], dis=True)
                                op(DVE, lambda: nc.vector.tensor_tensor(out=vb[:, tt, cc * 512:(cc + 1) * 512], in0=vf[:], in1=vgb[:, cc * 512:(cc + 1) * 512], op=ALU.mult),
                                   reads=[vftok, vgb_tok], writes=[vb_tok], dis=True)
                                if pn == 7 and tt == 4:
                                    bg = norm_gen(xsrc, xsrc_toks, blk + 1, gsc1, sh1, vt, HT, HT_tok) if blk + 1 < NB else next_phase_norm(gsc2, sh2, vt)
                                    step(bg)
                    op(DVE, lambda: nc.vector.tensor_reduce(out=ss8[:], in_=sspart[:], axis=AX.X, op=ALU.add), reads=[ss_tok], writes=[ss_tok])
                    op(DVE, lambda: nc.vector.tensor_scalar(out=ss8[:], in0=ss8[:], scalar1=1.0 / D, scalar2=EPS, op0=ALU.mult, op1=ALU.add),
                       reads=[ss_tok], writes=[ss_tok])
                    op(DVE, lambda: nc.vector.reciprocal(out=ss8[:], in_=ss8[:]), reads=[ss_tok], writes=[ss_tok])
                    op(ACT, lambda: nc.scalar.activation(out=ss8[:], in_=ss8[:], func=AF.Sqrt), reads=[ss_tok], writes=[ss_tok])
                    for tt in range(NT):
                        st = tt // 4
                        wp, wptok = wp_rot.next()
                        op(DVE, lambda: nc.vector.tensor_scalar(out=wp[:], in0=wsT[:], scalar1=ss8[:, tt:tt + 1], scalar2=None, op0=ALU.mult),
                           reads=[wsT_tok, ss_tok], writes=[wptok])
                        for gq in range(4):
                            ps, pstok = next_ps()
                            for gi in range(4):
                                g = gq * 4 + gi
                                gs = slice(g * 128, (g + 1) * 128)
                                o_ = ps[:, gi * 128:(gi + 1) * 128]
                                op(PE, lambda: nc.tensor.matmul(o_, lhsT=vb[:, tt, gs], rhs=wp[:, gs], start=True, stop=False),
                                   reads=[vb_tok, wptok], writes=[pstok], signal=False)
                                op(PE, lambda: nc.tensor.matmul(o_, lhsT=ones[0:1, 0:128], rhs=bhi[0:1, gs], start=False, stop=False),
                                   reads=[bs_tok], writes=[pstok], signal=False)
                                op(PE, lambda: nc.tensor.matmul(o_, lhsT=ones[0:1, 0:128], rhs=blo[0:1, gs], start=False, stop=True),
                                   reads=[bs_tok], writes=[pstok], signal=(gi == 3))
                            uv = uT[:, gq * 4:(gq + 1) * 4, tt * 128:(tt + 1) * 128]
                            op(DVE, lambda: nc.vector.tensor_tensor(out=uv, in0=uv, in1=ps[:].rearrange("p (a b) -> p a b", a=4), op=ALU.mult),
                               reads=[pstok, uT_tok[st]], writes=[uT_tok[st]], dis=True)
                            if gq == 0:
                                step(bg)
                    out_proj(ws, blk * 12 + 8, blk, uT, uT_tok, KC, g1c, vt, xsrc, xsrc_toks, bg=bg, bg_n=2)
                    exhaust(bg)
                K.barrier()
                xrot_cur[0] = xt_rot
        xsrc, xsrc_toks = XTL
        if stop_after == ("mix", li):
            break

        with ExitStack() as ph:
            M = ph.enter_context(sbt("M", [128, FC, TB], BF16))
            M_tok = [[GTok("M%d_%d" % (i, g_)) for g_ in range((FC + 7) // 8)] for i in range(NST)]
            NWP = 3
            slots = [(ph.enter_context(sbt("fpan%d" % i, [128, 16 * 512], BF16)), GTok("pan%d" % i)) for i in range(NWP)]
            cw = ph.enter_context(sbt("cw", [128, 3 * 2 * FC], F32))
            cb = ph.enter_context(sbt("cb", [128, 2 * FC], F32))
            carry = ph.enter_context(sbt("carry", [128, 2 * FC, 2], F32))
            CV = GTok("convvec")
            carry_tok = GTok("carry")
            cf_rot = Rot([(ph.enter_context(sbt("cf%d" % i, [128, 512], F32)), GTok("cf%d" % i)) for i in range(4)])
            gs_rot = Rot([(ph.enter_context(sbt("gs%d" % i, [128, 512], F32)), GTok("gs%d" % i)) for i in range(2)])
            dma(SP, cw[:], f_cw[li], CV, writes=[CV])
            dma(SP, cb[:], f_cb[li], CV, writes=[CV], dis=True)
            op(DVE, lambda: nc.vector.memset(carry[:], 0.0), writes=[carry_tok])
            wvi = wview(f_win[li])
            wvo = wview(f_wout[li])
            steps = [(c, min(2, FC - c)) for c in range(0, FC, 2)]

            def mk_a(stp):
                def f(wp, wptok):
                    c0, nch = stp
                    w = nch * 128
                    g3 = wp[:, 0:16 * 256].rearrange("p (k n) -> p k n", k=16)
                    u3 = wp[:, 16 * 256:32 * 256].rearrange("p (k n) -> p k n", k=16)
                    dma(POOL, g3[:, :, 0:w], wvi[:, :, c0 * 128:c0 * 128 + w], wptok, writes=[wptok])
                    dma(POOL, u3[:, :, 0:w], wvi[:, :, DFF + c0 * 128:DFF + c0 * 128 + w], wptok, writes=[wptok], dis=True)
                    return g3, u3
                return f

            def mk_b(n):
                def f(wp, wptok):
                    p3 = wp[:, 0:FC * 128].rearrange("p (k n) -> p k n", k=FC)
                    dma(POOL, p3[:, 0:16, :], wvo[:, 0:16, n * 128:(n + 1) * 128], wptok, writes=[wptok])
                    dma(POOL, p3[:, 16:32, :], wvo[:, 16:32, n * 128:(n + 1) * 128], wptok, writes=[wptok], dis=True)
                    dma(POOL, p3[:, 32:FC, :], wvo[:, 32:FC, n * 128:(n + 1) * 128], wptok, writes=[wptok], dis=True)
                    return p3
                return f

            loads = []
            for blk in range(NB):
                loads += [mk_a(stp) for stp in steps]
                loads += [mk_b(n) for n in range(KC)]
            per_blk = len(steps) + KC
            ws = WStream(slots, loads)
            ws.get(0)

            def conv(ps, pstok, cidx):
                cf, cftok = cf_rot.next()
                w0 = cw[:, cidx:cidx + 1]
                w1 = cw[:, 2 * FC + cidx:2 * FC + cidx + 1]
                w2 = cw[:, 4 * FC + cidx:4 * FC + cidx + 1]
                op(ACT, lambda: nc.scalar.activation(out=cf[:], in_=ps[:], func=AF.Identity, scale=w2, bias=cb[:, cidx:cidx + 1]),
                   reads=[pstok, CV], writes=[cftok])
                op(DVE, lambda: nc.vector.scalar_tensor_tensor(out=cf[:, 1:512], in0=ps[:, 0:511], scalar=w1, in1=cf[:, 1:512], op0=ALU.mult, op1=ALU.add),
                   reads=[pstok, cftok, CV], writes=[cftok])
                op(DVE, lambda: nc.vector.scalar_tensor_tensor(out=cf[:, 2:512], in0=ps[:, 0:510], scalar=w0, in1=cf[:, 2:512], op0=ALU.mult, op1=ALU.add),
                   reads=[pstok, cftok, CV], writes=[cftok])
                op(DVE, lambda: nc.vector.scalar_tensor_tensor(out=cf[:, 0:2], in0=carry[:, cidx, 0:2], scalar=w0, in1=cf[:, 0:2], op0=ALU.mult, op1=ALU.add),
                   reads=[carry_tok, cftok, CV], writes=[cftok])
                op(DVE, lambda: nc.vector.scalar_tensor_tensor(out=cf[:, 0:1], in0=carry[:, cidx, 1:2], scalar=w1, in1=cf[:, 0:1], op0=ALU.mult, op1=ALU.add),
                   reads=[carry_tok, cftok, CV], writes=[cftok])
                op(ACT, lambda: nc.scalar.activation(out=carry[:, cidx, 0:2], in_=ps[:, 510:512], func=AF.Copy),
                   reads=[pstok, carry_tok], writes=[carry_tok])
                return cf, cftok

            first_norm(gsc2, sh2, vt)
            for blk in range(NB):
                bg = None
                for si, stp in enumerate(steps):
                    if si == len(steps) - 1:
                        if blk + 1 < NB:
                            bg = norm_gen(xsrc, xsrc_toks, blk + 1, gsc2, sh2, vt, HT, HT_tok)
                        elif li + 1 < n_layers and (li + 1) in mod_done and stop_after is None:
                            l2 = li + 1
                            bg = next_phase_norm(gsc1all[:, l2 * KC:(l2 + 1) * KC], modall[:, l2 * 96:l2 * 96 + 16], VECL[l2])
                        step(bg)
                    (g3, u3), wptok = ws.get(blk * per_blk + si)
                    c0, nch = stp
                    for ci in range(nch):
                        c = c0 + ci
                        for st in range(NST):
                            psg, psgtok = next_ps()
                            psu, psutok = next_ps()
                            for kc in range(KC):
                                op(PE, lambda: nc.tensor.matmul(psg[:], lhsT=g3[:, kc, ci * 128:(ci + 1) * 128], rhs=HT[:, kc, st * 512:(st + 1) * 512],
                                                                start=(kc == 0), stop=(kc == KC - 1)),
                                   reads=[wptok, HT_tok[st]], writes=[psgtok], signal=(kc == KC - 1))
                            for kc in range(KC):
                                op(PE, lambda: nc.tensor.matmul(psu[:], lhsT=u3[:, kc, ci * 128:(ci + 1) * 128], rhs=HT[:, kc, st * 512:(st + 1) * 512],
                                                                start=(kc == 0), stop=(kc == KC - 1)),
                                   reads=[wptok, HT_tok[st]], writes=[psutok], signal=(kc == KC - 1))
                            cg, cgtok = conv(psg, psgtok, c)
                            cu, cutok = conv(psu, psutok, FC + c)
                            gs_, gstok = gs_rot.next()
                            op(ACT, lambda: nc.scalar.activation(out=gs_[:], in_=cg[:], func=AF.Silu), reads=[cgtok], writes=[gstok])
                            op(DVE, lambda: nc.vector.tensor_tensor(out=M[:, c, st * 512:(st + 1) * 512], in0=gs_[:], in1=cu[:], op=ALU.mult),
                               reads=[gstok, cutok], writes=[M_tok[st][c // 8]], dis=True)
                groups = [(n, st) for n in range(KC) for st in range(NST)]
                nxt = resid_load(0, blk * NST, xsrc, xsrc_toks)
                for gi, (n, st) in enumerate(groups):
                    p3, wptok = ws.get(blk * per_blk + len(steps) + n)
                    cur = nxt
                    if gi + 1 < len(groups):
                        nxt = resid_load(groups[gi + 1][0], blk * NST + groups[gi + 1][1], xsrc, xsrc_toks)
                    ps, pstok = next_ps()
                    for kc in range(FC):
                        op(PE, lambda: nc.tensor.matmul(ps[:], lhsT=p3[:, kc, :], rhs=M[:, kc, st * 512:(st + 1) * 512],
                                                        start=(kc == 0), stop=(kc == FC - 1)),
                           reads=[wptok, M_tok[st][kc // 8]], writes=[pstok], signal=(kc == FC - 1))
                    step(bg, 2)
                    resid_apply(ps, pstok, cur, n, blk * NST + st, g2c[:, n:n + 1], vt)
                exhaust(bg)
            K.barrier()
        if stop_after == ("ffn", li):
            break

    if do_final:
        with ExitStack() as ph:
            frot = Rot(list(nx_rot.items) + [(ph.enter_context(sbt("nxf%d" % i, [128, 512], F32)), GTok("nxf%d" % i)) for i in range(8)])
            for blk in range(NB):
                exhaust(norm_gen(xsrc, xsrc_toks, blk, fing_sb, None, None, to_out=True, rot=frot, PF=6))
            K.barrier()
    else:
        for sg in range(S // 512):
            for c in range(KC):
                xt, xtok = xt_rot.next()
                dma(SP, xt[:], xsrc[c * 128:(c + 1) * 128, sg * 512:(sg + 1) * 512], xtok, reads=[xsrc_toks[sg]], writes=[xtok])
                dma(SP, outT[c * 128:(c + 1) * 128, sg * 512:(sg + 1) * 512], xt[:], xtok, reads=[xtok])
    K.barrier()
    K.es.close()
    return nc


def _consts():
    ident = np.eye(128, dtype=np.float32)
    p = np.arange(128)[:, None]
    cidx = np.arange(512)[None, :]
    masks = np.zeros((128, 4, 512), np.float32)
    for jj in range(4):
        masks[:, jj, :] = np.where(cidx - p >= 128 * jj, 0.0, -30000.0)
    t = np.arange(128)[None, :]
    tri = (t >= p).astype(np.float32)
    tri = np.tile(tri[:, None, :], (1, H, 1)).reshape(128, H * 128)
    return ident, masks.reshape(128, 4 * 512), np.ascontiguousarray(tri)


def prep_inputs(inputs):
    f = lambda a: np.ascontiguousarray(np.asarray(a, dtype=np.float32))
    x = f(inputs["x"])
    c = f(inputs["c"])
    ident, masks, tri = _consts()
    shared = {
        "mod_w": f(inputs["mod_w"]),
        "mod_b": f(np.asarray(inputs["mod_b"]).reshape(4, 96, 128).transpose(2, 0, 1).reshape(128, 4 * 96)),
        "mixg": f(np.asarray(inputs["mix_norm_g"]).reshape(4, KC, 128).transpose(2, 0, 1).reshape(128, 4 * KC)),
        "ffng": f(np.asarray(inputs["ffn_norm_g"]).reshape(4, KC, 128).transpose(2, 0, 1).reshape(128, 4 * KC)),
        "a_win": f(inputs["attn_w_in"]),
        "a_bf": f(np.asarray(inputs["attn_b_f"]).reshape(2, H, 1)),
        "a_wo": f(inputs["attn_w_o"]),
        "g_win": f(inputs["gm_w_in"]),
        "g_vg": f(np.asarray(inputs["gm_v_g"]).reshape(2, 1, D)),
        "g_ws": f(np.asarray(inputs["gm_w_s"]).transpose(0, 3, 1, 2).reshape(2, 128, H * 128)),
        "g_bs": f(np.asarray(inputs["gm_b_s"]).reshape(2, 1, D)),
        "g_wo": f(inputs["gm_w_o"]),
        "f_win": f(inputs["ffn_w_in"]),
        "f_cw": f(np.asarray(inputs["ffn_conv_w"]).reshape(4, 3, 2 * FC, 128).transpose(0, 3, 1, 2).reshape(4, 128, 3 * 2 * FC)),
        "f_cb": f(np.asarray(inputs["ffn_conv_b"]).reshape(4, 2 * FC, 128).transpose(0, 2, 1)),
        "f_wout": f(inputs["ffn_w_out"]),
        "fin_g": f(np.asarray(inputs["final_g"]).reshape(KC, 128).T),
        "c_ident": ident, "c_masks": masks, "c_tri": tri,
    }
    in_maps = []
    for b in range(NCORES):
        m = dict(shared)
        m["xT"] = np.ascontiguousarray(x[b].T)
        m["c_t"] = np.ascontiguousarray(c[b].reshape(KC, 128).T)
        in_maps.append(m)
    return in_maps


def kernel(**inputs):
    in_maps = prep_inputs(inputs)
    nc = build()
    res = run_bass_kernel_spmd(nc, in_maps, core_ids=list(range(NCORES)))
    out = np.stack([np.ascontiguousarray(np.asarray(r["outT"]).T) for r in res.results], axis=0)
    return out.astype(np.float32)
```

```python
import numpy as np
from contextlib import ExitStack
import concourse.bass as bass
import concourse.mybir as mybir
from concourse.bass_utils import run_bass_kernel_spmd

F32 = mybir.dt.float32
BF16 = mybir.dt.bfloat16
AF = mybir.ActivationFunctionType
ALU = mybir.AluOpType
AX = mybir.AxisListType

D = 2048
S = 4096
KC = 16
DFF = 5504
FC = 43
H = 16
DH = 128
NST = 2
TB = 512 * NST
NB = S // TB
EPS = 1e-6
NCORES = 8
SAME_ENGINE_SYNC = True


class Sem:
    def __init__(self, h, dma):
        self.h = h
        self.dma = dma
        self.issued = 0


class Tok:
    def __init__(self, name):
        self.name = name
        self.w = {}
        self.r = {}
        self.sem = None
        self.pend = None


class Eng:
    def __init__(self, q, sem, name, inorder_safe=False):
        self.q = q
        self.sem = sem
        self.name = name
        self.known = {}
        self.pending = []
        self.inorder_safe = inorder_safe


class Ctx:
    def __init__(self):
        self.nc = bass.Bass("TRN2", target_bir_lowering=False)
        self.es = ExitStack()
        self.sems = []
        self.nsem = 0
        nc = self.nc
        self.PE = Eng(nc.tensor, self.new_sem(False), "pe", inorder_safe=True)
        self.ACT = Eng(nc.scalar, self.new_sem(False), "act")
        self.DVE = Eng(nc.vector, self.new_sem(False), "dve")
        self.POOL = Eng(nc.gpsimd, self.new_sem(False), "pool")
        self.SP = Eng(nc.sync, None, "sp")
        self.engs = [self.PE, self.ACT, self.DVE, self.POOL, self.SP]

    def new_sem(self, dma):
        self.nsem += 1
        h = self.es.enter_context(self.nc.semaphore("s%d" % self.nsem))
        s = Sem(h, dma)
        self.sems.append(s)
        return s

    def _deps(self, eng, reads, writes, dis):
        ev = {}

        def add(d):
            for k, e in d.items():
                if k not in ev or ev[k][1] < e[1]:
                    ev[k] = e

        for t in reads:
            if t.pend is not None and t.pend is not eng:
                raise RuntimeError("token %s has pending unsignaled access" % t.name)
            add(t.w)
        for t in writes:
            if t.pend is not None and t.pend is not eng:
                raise RuntimeError("token %s has pending unsignaled access" % t.name)
            add(t.r)
            if not dis:
                add(t.w)
        return ev

    def _wait(self, eng, ev):
        for k, (sm, v) in ev.items():
            if sm is eng.sem and (eng.inorder_safe or not SAME_ENGINE_SYNC):
                continue
            val = sm.issued if sm.dma else v
            if eng.known.get(k, 0) >= val:
                continue
            eng.q.wait_ge(sm.h, val)
            eng.known[k] = val

    @staticmethod
    def _record(event, reads, writes, dis):
        k = id(event[0])
        for t in reads:
            t.r[k] = event
        for t in writes:
            if dis:
                t.w[k] = event
            else:
                t.w = {k: event}
                t.r = {}

    def op(self, eng, fn, reads=(), writes=(), signal=True, dis=False):
        ev = self._deps(eng, reads, writes, dis)
        self._wait(eng, ev)
        inst = fn()
        if signal:
            eng.sem.issued += 1
            inst.then_inc(eng.sem.h, 1)
            event = (eng.sem, eng.sem.issued)
            for (r, w, d) in eng.pending:
                self._record(event, r, w, d)
                for t in list(r) + list(w):
                    t.pend = None
            eng.pending = []
            self._record(event, reads, writes, dis)
        else:
            eng.pending.append((tuple(reads), tuple(writes), dis))
            for t in list(reads) + list(writes):
                t.pend = eng
        return inst

    def dma(self, eng, out, in_, sbtok, reads=(), writes=(), dis=False):
        ev = self._deps(eng, reads, writes, dis)
        self._wait(eng, ev)
        if sbtok.sem is None:
            sbtok.sem = self.new_sem(True)
        sm = sbtok.sem
        eng.q.dma_start(out=out, in_=in_).then_inc(sm.h, 16)
        sm.issued += 16
        self._record((sm, sm.issued), reads, writes, dis)

    def barrier(self):
        for e in self.engs:
            assert not e.pending, "pending ops at barrier on %s" % e.name
        for e in self.engs:
            for sm in self.sems:
                if sm.issued == 0:
                    continue
                k = id(sm)
                if e.known.get(k, 0) >= sm.issued:
                    continue
                if sm is e.sem and e.inorder_safe:
                    continue
                e.q.wait_ge(sm.h, sm.issued)
                e.known[k] = sm.issued


class Rot:
    def __init__(self, items):
        self.items = items
        self.i = 0

    def next(self):
        it = self.items[self.i % len(self.items)]
        self.i += 1
        return it


class WStream:
    def __init__(self, slots, loads):
        self.slots = slots
        self.loads = loads
        self.views = [None] * len(loads)
        self.issued = 0

    def get(self, i):
        ns = len(self.slots)
        lim = min(len(self.loads), i + ns)
        while self.issued < lim:
            j = self.issued
            ap, tok = self.slots[j % ns]
            self.views[j] = (self.loads[j](ap, tok), tok)
            self.issued += 1
        return self.views[i]


def step(gen, n=1):
    if gen is None:
        return
    for _ in range(n):
        try:
            next(gen)
        except StopIteration:
            return


def exhaust(gen):
    if gen is None:
        return
    for _ in gen:
        pass


def build(n_layers=4, do_final=True, stop_after=None):
    K = Ctx()
    nc = K.nc
    PE, ACT, DVE, POOL, SP = K.PE, K.ACT, K.DVE, K.POOL, K.SP
    op, dma = K.op, K.dma

    def din(name, shape, dt=F32):
        return nc.dram_tensor(name, list(shape), dt, kind="ExternalInput").ap()

    xT_in = din("xT", [D, S])
    c_t = din("c_t", [128, KC])
    mod_w = din("mod_w", [4, D, 6 * D])
    mod_b = din("mod_b", [128, 4 * 96])
    mixg = din("mixg", [128, 4 * KC])
    ffng = din("ffng", [128, 4 * KC])
    a_win = din("a_win", [2, D, 3 * D + H])
    a_bf = din("a_bf", [2, H, 1])
    a_wo = din("a_wo", [2, D, D])
    g_win = din("g_win", [2, D, 2 * D])
    g_vg = din("g_vg", [2, 1, D])
    g_ws = din("g_ws", [2, 128, H * 128])
    g_bs = din("g_bs", [2, 1, D])
    g_wo = din("g_wo", [2, D, D])
    f_win = din("f_win", [4, D, 2 * DFF])
    f_cw = din("f_cw", [4, 128, 3 * 2 * FC])
    f_cb = din("f_cb", [4, 128, 2 * FC])
    f_wout = din("f_wout", [4, DFF, D])
    fin_g = din("fin_g", [128, KC])
    c_ident = din("c_ident", [128, 128])
    c_masks = din("c_masks", [128, 4 * 512])
    c_tri = din("c_tri", [128, H * 128])
    outT = nc.dram_tensor("outT", [D, S], F32, kind="ExternalOutput").ap()

    XT = nc.dram_tensor("XT", [D, S], F32).ap()
    QT = nc.dram_tensor("QT", [D, S], BF16).ap()
    KT = nc.dram_tensor("KT", [D, S], BF16).ap()
    OT = nc.dram_tensor("OT", [D, S], BF16).ap()
    VV = nc.dram_tensor("VV", [S, D], BF16).ap()
    FR = nc.dram_tensor("FR", [6, H, S], BF16).ap()

    XT_tok = [Tok("XT%d" % i) for i in range(S // 512)]
    QT_tok, KT_tok, VV_tok, OT_tok, FR_tok = Tok("QT"), Tok("KT"), Tok("VV"), Tok("OT"), Tok("FR")
    NOTOK = Tok("ro")
    _gt = {}
    _uid = [0]

    def sbt(name, shape, dt):
        _uid[0] += 1
        return nc.sbuf_tensor("%s_u%d" % (name, _uid[0]), shape, dt)

    def GTok(name):
        if name not in _gt:
            _gt[name] = Tok(name)
        return _gt[name]

    es = K.es

    def sb(name, shape, dt):
        return es.enter_context(nc.sbuf_tensor(name, list(shape), dt))

    HT = sb("HT", [128, KC, TB], BF16)
    HT_tok = [Tok("HT%d" % i) for i in range(NST)]
    nx_rot = Rot([(sb("nx%d" % i, [128, 512], F32), Tok("nx%d" % i)) for i in range(4)])
    xt_rot = Rot([(sb("xt%d" % i, [128, 512], F32), Tok("xt%d" % i)) for i in range(3)])
    sq_rot = Rot([(sb("sq%d" % i, [128, 512], BF16), Tok("sq%d" % i)) for i in range(2)])
    rstd_rot = Rot([(sb("rstd%d" % i, [128, 512], F32), Tok("rstd%d" % i)) for i in range(1)])
    modall = sb("modall", [128, 4 * 96], F32)
    modb_sb = sb("modb_sb", [128, 4 * 96], F32)
    gsc1all = sb("gsc1all", [128, 4 * KC], F32)
    gsc2all = sb("gsc2all", [128, 4 * KC], F32)
    mixg_sb = sb("mixg_sb", [128, 4 * KC], F32)
    ffng_sb = sb("ffng_sb", [128, 4 * KC], F32)
    fing_sb = sb("fing_sb", [128, KC], F32)
    cin = sb("cin", [128, KC], F32)
    cact = sb("cact", [128, KC], BF16)
    ident = sb("ident", [128, 128], BF16)
    ones = sb("ones", [128, 128], BF16)
    VECL = [Tok("vec%d" % i) for i in range(4)]
    CONST = Tok("const")
    PS = [es.enter_context(nc.psum_tensor("ps%d" % i, [128, 512], F32)) for i in range(8)]
    PS_tok = [Tok("ps%d" % i) for i in range(8)]
    ps_rr = [0]

    def next_ps():
        i = ps_rr[0] % 7
        ps_rr[0] += 1
        return PS[i], PS_tok[i]

    dma(POOL, ident[:], c_ident, CONST, writes=[CONST], dis=True)
    op(DVE, lambda: nc.vector.memset(ones[:], 1.0), writes=[CONST], dis=True)
    dma(SP, cin[:], c_t, CONST, writes=[CONST], dis=True)
    dma(SP, fing_sb[:], fin_g, CONST, writes=[CONST], dis=True)
    dma(SP, modb_sb[:], mod_b, CONST, writes=[CONST], dis=True)
    dma(SP, mixg_sb[:], mixg, CONST, writes=[CONST], dis=True)
    dma(SP, ffng_sb[:], ffng, CONST, writes=[CONST], dis=True)
    op(ACT, lambda: nc.scalar.activation(out=cact[:], in_=cin[:], func=AF.Silu), reads=[CONST], writes=[CONST], dis=True)
    K.barrier()

    def wview(w2d):
        return w2d.rearrange("(kc p) n -> p kc n", p=128)

    def load_panel(dst3, tok, wv, kc0, kc1, n0, n1):
        k = kc0
        while k < kc1:
            ke = min(k + 16, kc1)
            dma(POOL, dst3[:, k - kc0:ke - kc0, 0:n1 - n0], wv[:, k:ke, n0:n1], tok, writes=[tok], dis=(k > kc0))
            k = ke
        return dst3

    def panel_loads(wv, nk, cols):
        return [(lambda ap, tok, n0=n0: load_panel(ap, tok, wv, 0, nk, n0, n0 + 512)) for n0 in cols]

    PF = 2

    def norm_gen(xsrc, xsrc_toks, blk, gcol, bcol, vtok, HTd=None, HTd_tok=None, to_out=False, rot=None, PF=2):
        pss, pss_tok = PS[7], PS_tok[7]
        seq = [(st, p, c) for st in range(NST) for p in (1, 2) for c in range(KC)]
        tiles = {}

        def ld(k):
            st, p, c = seq[k]
            t0 = (blk * NST + st) * 512
            xt, xtok = (rot or nx_rot).next()
            dma(SP, xt[:], xsrc[c * 128:(c + 1) * 128, t0:t0 + 512], xtok, reads=[xsrc_toks[blk * NST + st]], writes=[xtok])
            tiles[k] = (xt, xtok)

        for k in range(min(PF, len(seq))):
            ld(k)
        yield
        rs = rstok = None
        for k, (st, p, c) in enumerate(seq):
            if k + PF < len(seq):
                ld(k + PF)
            t0 = (blk * NST + st) * 512
            xt, xtok = tiles.pop(k)
            if p == 1:
                sq, sqtok = sq_rot.next()
                op(ACT, lambda: nc.scalar.activation(out=sq[:], in_=xt[:], func=AF.Square), reads=[xtok], writes=[sqtok])
                op(PE, lambda: nc.tensor.matmul(pss[:], lhsT=ones[:, 0:128], rhs=sq[:], start=(c == 0), stop=(c == KC - 1)),
                   reads=[sqtok], writes=[pss_tok], signal=True)
                if c == KC - 1:
                    rs, rstok = rstd_rot.next()
                    op(DVE, lambda: nc.vector.tensor_scalar(out=rs[:], in0=pss[:], scalar1=1.0 / D, scalar2=EPS, op0=ALU.mult, op1=ALU.add),
                       reads=[pss_tok], writes=[rstok])
                    op(DVE, lambda: nc.vector.reciprocal(out=rs[:], in_=rs[:]), reads=[rstok], writes=[rstok])
                    op(ACT, lambda: nc.scalar.activation(out=rs[:], in_=rs[:], func=AF.Sqrt), reads=[rstok], writes=[rstok])
            else:
                op(DVE, lambda: nc.vector.tensor_tensor(out=xt[:], in0=xt[:], in1=rs[:], op=ALU.mult), reads=[xtok, rstok], writes=[xtok])
                if to_out:
                    op(ACT, lambda: nc.scalar.activation(out=xt[:], in_=xt[:], func=AF.Identity, scale=gcol[:, c:c + 1]),
                       reads=[xtok], writes=[xtok])
                    dma(SP, outT[c * 128:(c + 1) * 128, t0:t0 + 512], xt[:], xtok, reads=[xtok])
                else:
                    op(ACT, lambda: nc.scalar.activation(out=HTd[:, c, st * 512:(st + 1) * 512], in_=xt[:], func=AF.Identity,
                                                         scale=gcol[:, c:c + 1], bias=bcol[:, c:c + 1]),
                       reads=[xtok, vtok], writes=[HTd_tok[st]], dis=True)
            yield

    xrot_cur = [xt_rot]

    def extra_xt(ph, n):
        items = list(xt_rot.items) + [(ph.enter_context(sbt("xtx%d" % i, [128, 512], F32)), GTok("xtx%d" % i)) for i in range(n)]
        xrot_cur[0] = Rot(items)

    def resid_load(n, sg, xsrc, xsrc_toks):
        t0 = sg * 512
        xt, xtok = xrot_cur[0].next()
        dma(SP, xt[:], xsrc[n * 128:(n + 1) * 128, t0:t0 + 512], xtok, reads=[xsrc_toks[sg]], writes=[xtok])
        return xt, xtok

    def resid_apply(ps, pstok, xtile, n, sg, gcolumn, vtok):
        t0 = sg * 512
        xt, xtok = xtile
        op(DVE, lambda: nc.vector.scalar_tensor_tensor(out=xt[:], in0=ps[:], scalar=gcolumn, in1=xt[:], op0=ALU.mult, op1=ALU.add),
           reads=[pstok, xtok, vtok], writes=[xtok])
        dma(SP, XT[n * 128:(n + 1) * 128, t0:t0 + 512], xt[:], xtok, reads=[xtok], writes=[XT_tok[sg]], dis=True)

    def out_proj(ws, wbase, blk, rhs3, rhs_toks, nk, gcols, vtok, xsrc, xsrc_toks, bg=None, bg_n=1):
        groups = [(pn, q, st) for pn in range(D // 512) for q in range(4) for st in range(NST)]
        nxt = resid_load(groups[0][0] * 4 + groups[0][1], blk * NST + groups[0][2], xsrc, xsrc_toks)
        for gi, (pn, q, st) in enumerate(groups):
            p3, ptok = ws.get(wbase + pn)
            n = pn * 4 + q
            cur = nxt
            if gi + 1 < len(groups):
                pn2, q2, st2 = groups[gi + 1]
                nxt = resid_load(pn2 * 4 + q2, blk * NST + st2, xsrc, xsrc_toks)
            ps, pstok = next_ps()
            for kc in range(nk):
                op(PE, lambda: nc.tensor.matmul(ps[:], lhsT=p3[:, kc, q * 128:(q + 1) * 128],
                                                rhs=rhs3[:, kc, st * 512:(st + 1) * 512],
                                                start=(kc == 0), stop=(kc == nk - 1)),
                   reads=[ptok, rhs_toks[st]], writes=[pstok], signal=(kc == nk - 1))
            resid_apply(ps, pstok, cur, n, blk * NST + st, gcols[:, n:n + 1], vtok)
            step(bg, bg_n)

    def mod_gen(li, slots):
        wv = wview(mod_w[li])
        psm, psm_tok = PS[7], PS_tok[7]
        ws = WStream(slots, panel_loads(wv, KC, [pn * 512 for pn in range(6 * D // 512)]))
        for pn in range(6 * D // 512):
            p3, ptok = ws.get(pn)
            for q in range(4):
                n = pn * 4 + q
                for kc in range(KC):
                    op(PE, lambda: nc.tensor.matmul(psm[:, n:n + 1], lhsT=p3[:, kc, q * 128:(q + 1) * 128], rhs=cact[:, kc:kc + 1],
                                                    start=(kc == 0), stop=(kc == KC - 1)),
                       reads=[ptok], writes=[psm_tok], signal=(kc == KC - 1))
                yield
        m0 = li * 96
        vt = VECL[li]
        op(DVE, lambda: nc.vector.tensor_tensor(out=modall[:, m0:m0 + 96], in0=psm[:, 0:96], in1=modb_sb[:, m0:m0 + 96], op=ALU.add),
           reads=[psm_tok], writes=[vt])
        op(DVE, lambda: nc.vector.scalar_tensor_tensor(out=gsc1all[:, li * KC:(li + 1) * KC], in0=modall[:, m0 + 16:m0 + 32], scalar=1.0,
                                                       in1=mixg_sb[:, li * KC:(li + 1) * KC], op0=ALU.add, op1=ALU.mult),
           reads=[vt], writes=[vt])
        op(DVE, lambda: nc.vector.scalar_tensor_tensor(out=gsc2all[:, li * KC:(li + 1) * KC], in0=modall[:, m0 + 64:m0 + 80], scalar=1.0,
                                                       in1=ffng_sb[:, li * KC:(li + 1) * KC], op0=ALU.add, op1=ALU.mult),
           reads=[vt], writes=[vt])
        yield

    xsrc, xsrc_toks = xT_in, [NOTOK] * (S // 512)
    XTL = [XT, XT_tok]
    mod_done = set()
    prenorm = [False]

    def first_norm(gcol, bcol, vtok_):
        if prenorm[0]:
            prenorm[0] = False
            return
        exhaust(norm_gen(xsrc, xsrc_toks, 0, gcol, bcol, vtok_, HT, HT_tok))

    def next_phase_norm(gcol, bcol, vtok_):
        prenorm[0] = True
        return norm_gen(XT, XT_tok, 0, gcol, bcol, vtok_, HT, HT_tok)

    def chain(gens):
        for g in gens:
            yield from g

    for li in range(n_layers):
        j = li // 2
        if li not in mod_done:
            with ExitStack() as ph:
                slots = [(ph.enter_context(sbt("mpan%d" % i, [128, KC, 512], BF16)), GTok("mpan%d" % i)) for i in range(3)]
                exhaust(mod_gen(li, slots))
                mod_done.add(li)
                K.barrier()
        vt = VECL[li]
        m0 = li * 96
        sh1 = modall[:, m0 + 0:m0 + 16]
        g1c = modall[:, m0 + 32:m0 + 48]
        sh2 = modall[:, m0 + 48:m0 + 64]
        g2c = modall[:, m0 + 80:m0 + 96]
        gsc1 = gsc1all[:, li * KC:(li + 1) * KC]
        gsc2 = gsc2all[:, li * KC:(li + 1) * KC]

        if li % 2 == 0:
            with ExitStack() as ph:
                fpan = ph.enter_context(sbt("fpan", [128, KC, H], BF16))
                fpan_tok = GTok("fpan")
                FN = ph.enter_context(sbt("FN", [H, S], F32))
                FN_tok = GTok("FN")
                ftmp = ph.enter_context(sbt("ftmp", [H, 512], F32))
                ftmp_tok = GTok("ftmp")
                nbf = ph.enter_context(sbt("nbf", [H, 1], F32))
                nbf_tok = GTok("nbf")
                pha = ExitStack()
                slots = [(pha.enter_context(sbt("qpan%d" % i, [128, KC, 512], BF16)), GTok("pan%d" % i)) for i in range(3)]
                HT2 = pha.enter_context(sbt("HT2", [128, KC, TB], BF16))
                HT2_tok = [GTok("HT2_%d" % i) for i in range(NST)]
                HTs = [(HT, HT_tok), (HT2, HT2_tok)]
                stg_rot = Rot([(pha.enter_context(sbt("stg%d" % i, [128, 512], BF16)), GTok("stg%d" % i)) for i in range(4)])
                wv = wview(a_win[j])
                ws = WStream(slots, panel_loads(wv, KC, [pn * 512 for pn in range(12)] * NB))
                ws.get(0)
                dma(POOL, fpan[:], wv[:, :, 3 * D:3 * D + H], fpan_tok, writes=[fpan_tok])
                dma(SP, nbf[:], a_bf[j], nbf_tok, writes=[nbf_tok])
                op(DVE, lambda: nc.vector.tensor_scalar(out=nbf[:], in0=nbf[:], scalar1=-1.0, scalar2=None, op0=ALU.mult),
                   reads=[nbf_tok], writes=[nbf_tok])
                first_norm(gsc1, sh1, vt)
                for blk in range(NB):
                    Hc, Hc_tok = HTs[blk % 2]
                    bg = None
                    if blk + 1 < NB:
                        Hn, Hn_tok = HTs[(blk + 1) % 2]
                        bg = norm_gen(xsrc, xsrc_toks, blk + 1, gsc1, sh1, vt, Hn, Hn_tok)
                    for pn in range(12):
                        p3, ptok = ws.get(blk * 12 + pn)
                        if pn < 8:
                            for q in range(4):
                                n = (pn % 4) * 4 + q
                                for st in range(NST):
                                    ps, pstok = next_ps()
                                    for kc in range(KC):
                                        op(PE, lambda: nc.tensor.matmul(ps[:], lhsT=p3[:, kc, q * 128:(q + 1) * 128],
                                                                        rhs=Hc[:, kc, st * 512:(st + 1) * 512],
                                                                        start=(kc == 0), stop=(kc == KC - 1)),
                                           reads=[ptok, Hc_tok[st]], writes=[pstok], signal=(kc == KC - 1))
                                    sg_, sgtok = stg_rot.next()
                                    t0 = (blk * NST + st) * 512
                                    if pn < 4:
                                        op(ACT, lambda: nc.scalar.activation(out=sg_[:], in_=ps[:], func=AF.Copy, scale=float(DH ** -0.5)),
                                           reads=[pstok], writes=[sgtok])
                                        dma(SP, QT[n * 128:(n + 1) * 128, t0:t0 + 512], sg_[:], sgtok, reads=[sgtok], writes=[QT_tok], dis=True)
                                    else:
                                        op(DVE, lambda: nc.vector.tensor_copy(out=sg_[:], in_=ps[:]), reads=[pstok], writes=[sgtok])
                                        dma(SP, KT[n * 128:(n + 1) * 128, t0:t0 + 512], sg_[:], sgtok, reads=[sgtok], writes=[KT_tok], dis=True)
                                    step(bg)
                        else:
                            cc = pn - 8
                            for tt in range(TB // 128):
                                st = tt // 4
                                ps, pstok = next_ps()
                                for kc in range(KC):
                                    op(PE, lambda: nc.tensor.matmul(ps[:], lhsT=Hc[:, kc, tt * 128:(tt + 1) * 128], rhs=p3[:, kc, :],
                                                                    start=(kc == 0), stop=(kc == KC - 1)),
                                       reads=[ptok, Hc_tok[st]], writes=[pstok], signal=(kc == KC - 1))
                                sg_, sgtok = stg_rot.next()
                                tk0 = blk * TB + tt * 128
                                if tt % 2 == 0:
                                    op(ACT, lambda: nc.scalar.activation(out=sg_[:], in_=ps[:], func=AF.Copy), reads=[pstok], writes=[sgtok])
                                else:
                                    op(DVE, lambda: nc.vector.tensor_copy(out=sg_[:], in_=ps[:]), reads=[pstok], writes=[sgtok])
                                dma(SP, VV[tk0:tk0 + 128, cc * 512:(cc + 1) * 512], sg_[:], sgtok, reads=[sgtok], writes=[VV_tok], dis=True)
                                step(bg)
                    for st in range(NST):
                        ps, pstok = next_ps()
                        for kc in range(KC):
                            op(PE, lambda: nc.tensor.matmul(ps[0:H, :], lhsT=fpan[:, kc, :], rhs=Hc[:, kc, st * 512:(st + 1) * 512],
                                                            start=(kc == 0), stop=(kc == KC - 1)),
                               reads=[fpan_tok, Hc_tok[st]], writes=[pstok], signal=(kc == KC - 1))
                        t0 = (blk * NST + st) * 512
                        op(ACT, lambda: nc.scalar.activation(out=ftmp[:], in_=ps[0:H, :], func=AF.Exp, scale=-1.0, bias=nbf[:, 0:1]),
                           reads=[pstok, nbf_tok], writes=[ftmp_tok])
                        op(ACT, lambda: nc.scalar.activation(out=FN[:, t0:t0 + 512], in_=ftmp[:], func=AF.Ln, bias=1.0),
                           reads=[ftmp_tok], writes=[FN_tok], dis=True)
                    exhaust(bg)
                K.barrier()
                pha.close()
                G = ph.enter_context(sbt("G", [H, S], F32))
                R1 = ph.enter_context(sbt("R1", [H, S], F32))
                spl = [ph.enter_context(sbt("spl%d" % i, [H, S], BF16)) for i in range(6)]
                GT = GTok("G")
                op(DVE, lambda: nc.vector.tensor_tensor_scan(out=G[:], data0=FN[:], data1=FN[:], initial=0.0, op0=ALU.add, op1=ALU.bypass),
                   reads=[FN_tok], writes=[GT])
                op(DVE, lambda: nc.vector.tensor_copy(out=spl[0][:], in_=G[:]), reads=[GT], writes=[GT], dis=True)
                op(DVE, lambda: nc.vector.tensor_tensor(out=R1[:], in0=G[:], in1=spl[0][:], op=ALU.subtract), reads=[GT], writes=[GT], dis=True)
                op(DVE, lambda: nc.vector.tensor_copy(out=spl[1][:], in_=R1[:]), reads=[GT], writes=[GT], dis=True)
                op(DVE, lambda: nc.vector.tensor_tensor(out=G[:], in0=R1[:], in1=spl[1][:], op=ALU.subtract), reads=[GT], writes=[GT], dis=True)
                op(DVE, lambda: nc.vector.tensor_copy(out=spl[2][:], in_=G[:]), reads=[GT], writes=[GT], dis=True)
                for i in range(3):
                    op(DVE, lambda: nc.vector.tensor_scalar(out=spl[3 + i][:], in0=spl[i][:], scalar1=-1.0, scalar2=None, op0=ALU.mult),
                       reads=[GT], writes=[GT], dis=True)
                for i in range(6):
                    dma(SP, FR[i], spl[i][:], GT, reads=[GT], writes=[FR_tok], dis=True)
                K.barrier()

            with ExitStack() as ph:
                sets = []
                for i in range(2):
                    d = dict(
                        q=ph.enter_context(sbt("aq%d" % i, [128, S], BF16)),
                        k=ph.enter_context(sbt("ak%d" % i, [128, S], BF16)),
                        v=ph.enter_context(sbt("av%d" % i, [128, S // 128, DH], BF16)),
                        rk=ph.enter_context(sbt("ark%d" % i, [6, S], BF16)),
                        rq=ph.enter_context(sbt("arq%d" % i, [6, S], BF16)),
                        tok=GTok("aset%d" % i))
                    sets.append(d)
                masks = ph.enter_context(sbt("masks", [128, 4, 512], BF16))
                mtok = GTok("masks")
                dma(POOL, masks[:], c_masks.rearrange("p (a b) -> p a b", a=4), mtok, writes=[mtok])
                pt_rot = Rot([(ph.enter_context(sbt("pt%d" % i, [128, 512], BF16)), GTok("pt%d" % i)) for i in range(4)])
                rl_rot = Rot([(ph.enter_context(sbt("rl%d" % i, [128, 512], F32)), GTok("rl%d" % i)) for i in range(2)])
                stg_rot = Rot([(ph.enter_context(sbt("stg%d" % i, [128, 512], BF16)), GTok("stg%d" % i)) for i in range(4)])
                mslots = [(ph.enter_context(sbt("mpan%d" % i, [128, KC, 512], BF16)), GTok("mpan%d" % i)) for i in range(3)]
                todo = [l for l in (li + 1, li + 2) if l < n_layers and l not in mod_done]
                bg = chain([mod_gen(l, mslots) for l in todo]) if todo else None
                bg_n = 2 if len(todo) > 1 else 1
                for l in todo:
                    mod_done.add(l)
                for d in sets:
                    op(DVE, lambda: nc.vector.memset(d["rk"][:], 1.0), writes=[d["tok"]], dis=True)
                    op(DVE, lambda: nc.vector.memset(d["rq"][:], 1.0), writes=[d["tok"]], dis=True)
                vview = VV.rearrange("(kt p) d -> p kt d", p=128)

                def load_head(h, d):
                    tk = d["tok"]
                    dma(SP, d["q"][:], QT[h * 128:(h + 1) * 128, :], tk, reads=[QT_tok], writes=[tk], dis=False)
                    dma(SP, d["k"][:], KT[h * 128:(h + 1) * 128, :], tk, reads=[KT_tok], writes=[tk], dis=True)
                    dma(SP, d["v"][:, 0:16, :], vview[:, 0:16, h * 128:(h + 1) * 128], tk, reads=[VV_tok], writes=[tk], dis=True)
                    dma(SP, d["v"][:, 16:32, :], vview[:, 16:32, h * 128:(h + 1) * 128], tk, reads=[VV_tok], writes=[tk], dis=True)
                    dma(SP, d["rk"][0:3, :], FR[0:3, h, :], tk, reads=[FR_tok], writes=[tk], dis=True)
                    dma(SP, d["rq"][3:6, :], FR[3:6, h, :], tk, reads=[FR_tok], writes=[tk], dis=True)

                load_head(0, sets[0])
                SB_ = [0, 1, 2]
                for h in range(H):
                    d = sets[h % 2]
                    if h + 1 < H:
                        load_head(h + 1, sets[(h + 1) % 2])
                    tk = d["tok"]
                    for Q in range(S // 512):
                        nk = 4 * Q + 4
                        po, potok = PS[3 + Q % 2], PS_tok[3 + Q % 2]
                        pl, pltok = PS[5 + Q % 2], PS_tok[5 + Q % 2]
                        qs = slice(Q * 512, (Q + 1) * 512)

                        def cols(jk):
                            return 128 * (jk - 4 * Q) if jk >= 4 * Q else 0

                        def qk(jk):
                            ps, pstok = PS[SB_[jk % 3]], PS_tok[SB_[jk % 3]]
                            ks = slice(jk * 128, (jk + 1) * 128)
                            diag = jk >= 4 * Q
                            c0 = cols(jk)
                            qsl = slice(Q * 512 + c0, (Q + 1) * 512)
                            op(PE, lambda: nc.tensor.matmul(ps[:, c0:512], lhsT=d["k"][:, ks], rhs=d["q"][:, qsl], start=True, stop=False),
                               reads=[tk], writes=[pstok], signal=False)
                            if diag:
                                op(PE, lambda: nc.tensor.matmul(ps[:, c0:c0 + 128], lhsT=ident[:, :], rhs=masks[:, 0, 0:128], start=False, stop=False),
                                   reads=[mtok], writes=[pstok], signal=False)
                            op(PE, lambda: nc.tensor.matmul(ps[:, c0:512], lhsT=d["rk"][0:6, ks], rhs=d["rq"][0:6, qsl], start=False, stop=True),
                               reads=[tk], writes=[pstok], signal=True)

                        qk(0)
                        for jk in range(nk):
                            if jk + 1 < nk:
                                qk(jk + 1)
                            ps, pstok = PS[SB_[jk % 3]], PS_tok[SB_[jk % 3]]
                            c0 = cols(jk)
                            pt, pttok = pt_rot.next()
                            op(ACT, lambda: nc.scalar.activation(out=pt[:, c0:512], in_=ps[:, c0:512], func=AF.Exp), reads=[pstok], writes=[pttok])
                            op(PE, lambda: nc.tensor.matmul(po[:, c0:512], lhsT=d["v"][:, jk, :], rhs=pt[:, c0:512], start=(jk == 0), stop=(jk == nk - 1)),
                               reads=[tk, pttok], writes=[potok], signal=False)
                            op(PE, lambda: nc.tensor.matmul(pl[:, c0:512], lhsT=ones[:, :], rhs=pt[:, c0:512], start=(jk == 0), stop=(jk == nk - 1)),
                               reads=[pttok], writes=[pltok], signal=True)
                        rl, rltok = rl_rot.next()
                        op(DVE, lambda: nc.vector.reciprocal(out=rl[:], in_=pl[:]), reads=[pltok], writes=[rltok])
                        sg_, sgtok = stg_rot.next()
                        op(DVE, lambda: nc.vector.tensor_tensor(out=sg_[:], in0=po[:], in1=rl[:], op=ALU.mult),
                           reads=[potok, rltok], writes=[sgtok])
                        dma(SP, OT[h * 128:(h + 1) * 128, qs], sg_[:], sgtok, reads=[sgtok], writes=[OT_tok], dis=True)
                        step(bg, bg_n)
                exhaust(bg)
                K.barrier()

            with ExitStack() as ph:
                slots = [(ph.enter_context(sbt("opan%d" % i, [128, KC, 512], BF16)), GTok("opan%d" % i)) for i in range(4)]
                HT2 = ph.enter_context(sbt("HT2", [128, KC, TB], BF16))
                HT2_tok = [GTok("HT2_%d" % i) for i in range(NST)]
                HTs = [(HT, HT_tok), (HT2, HT2_tok)]
                extra_xt(ph, 5)
                otv = OT.rearrange("(kc p) t -> p kc t", p=128)

                class _Res:
                    def __init__(self):
                        wvo_ = wview(a_wo[j])
                        self.v = [(load_panel(ap, tok, wvo_, 0, KC, pn * 512, (pn + 1) * 512), tok) for pn, (ap, tok) in enumerate(slots)]

                    def get(self, i):
                        return self.v[i % 4]

                ws = _Res()

                def load_o(blk):
                    Hd, Hd_tok = HTs[blk % 2]
                    for st in range(NST):
                        t0 = (blk * NST + st) * 512
                        dma(SP, Hd[:, :, st * 512:(st + 1) * 512], otv[:, :, t0:t0 + 512], Hd_tok[st], reads=[OT_tok], writes=[Hd_tok[st]])

                load_o(0)
                for blk in range(NB):
                    if blk + 1 < NB:
                        load_o(blk + 1)
                    Hc, Hc_tok = HTs[blk % 2]
                    bgn = None
                    if blk == NB - 1 and Hc is not HT:
                        bgn = next_phase_norm(gsc2, sh2, vt)
                    out_proj(ws, blk * 4, blk, Hc, Hc_tok, KC, g1c, vt, xsrc, xsrc_toks, bg=bgn, bg_n=2)
                    exhaust(bgn)
                K.barrier()
                xrot_cur[0] = xt_rot
        else:
            with ExitStack() as ph:
                wsT = ph.enter_context(sbt("wsT", [128, H * 128], BF16))
                wsT_tok = GTok("wsT")
                vgb = ph.enter_context(sbt("vgb", [128, D], F32))
                vgb_tok = GTok("vgb")
                bhi = ph.enter_context(sbt("bhi", [1, D], BF16))
                blo = ph.enter_context(sbt("blo", [1, D], BF16))
                bs_tok = GTok("bs")
                with ExitStack() as ph2:
                    bsf = ph2.enter_context(sbt("bsf", [1, D], F32))
                    wsf = ph2.enter_context(sbt("wsf", [128, H * 128], F32))
                    tri = ph2.enter_context(sbt("tri", [128, H * 128], F32))
                    dma(SP, wsf[:], g_ws[j], wsT_tok, writes=[wsT_tok])
                    dma(SP, tri[:], c_tri, wsT_tok, writes=[wsT_tok], dis=True)
                    op(DVE, lambda: nc.vector.tensor_tensor(out=wsT[:], in0=wsf[:], in1=tri[:], op=ALU.mult), reads=[wsT_tok], writes=[wsT_tok], dis=True)
                    dma(SP, vgb[:], g_vg[j].partition_broadcast(128), vgb_tok, writes=[vgb_tok])
                    dma(SP, bsf[:], g_bs[j], bs_tok, writes=[bs_tok])
                    op(DVE, lambda: nc.vector.tensor_copy(out=bhi[:], in_=bsf[:]), reads=[bs_tok], writes=[bs_tok], dis=True)
                    op(DVE, lambda: nc.vector.tensor_tensor(out=bsf[:], in0=bsf[:], in1=bhi[:], op=ALU.subtract), reads=[bs_tok], writes=[bs_tok], dis=True)
                    op(DVE, lambda: nc.vector.tensor_copy(out=blo[:], in_=bsf[:]), reads=[bs_tok], writes=[bs_tok], dis=True)
                    K.barrier()
                slots = [(ph.enter_context(sbt("gpan%d" % i, [128, KC, 512], BF16)), GTok("pan%d" % i)) for i in range(3)]
                vb = ph.enter_context(sbt("vb", [128, TB // 128, D], BF16))
                vb_tok = GTok("vb")
                uT = ph.enter_context(sbt("uT", [128, KC, TB], BF16))
                uT_tok = [GTok("uT%d" % i) for i in range(NST)]
                wp_rot = Rot([(ph.enter_context(sbt("wp%d" % i, [128, H * 128], BF16)), GTok("wp%d" % i)) for i in range(2)])
                vf_rot = Rot([(ph.enter_context(sbt("vf%d" % i, [128, 512], F32)), GTok("vf%d" % i)) for i in range(2)])
                sspart = ph.enter_context(sbt("sspart", [128, TB // 128, 4], F32))
                ss8 = ph.enter_context(sbt("ss8", [128, TB // 128], F32))
                ss_tok = GTok("ss")
                extra_xt(ph, 3)
                wvi = wview(g_win[j])
                wvo = wview(g_wo[j])
                loads = []
                for blk in range(NB):
                    loads += panel_loads(wvi, KC, [pn * 512 for pn in range(8)])
                    loads += panel_loads(wvo, KC, [pn * 512 for pn in range(4)])
                ws = WStream(slots, loads)
                ws.get(0)
                NT = TB // 128
                first_norm(gsc1, sh1, vt)
                for blk in range(NB):
                    op(DVE, lambda: nc.vector.memset(sspart[:], 0.0), writes=[ss_tok])
                    for pn in range(8):
                        p3, ptok = ws.get(blk * 12 + pn)
                        if pn < 4:
                            for q in range(4):
                                g = pn * 4 + q
                                for st in range(NST):
                                    ps, pstok = next_ps()
                                    for kc in range(KC):
                                        op(PE, lambda: nc.tensor.matmul(ps[:], lhsT=p3[:, kc, q * 128:(q + 1) * 128],
                                                                        rhs=HT[:, kc, st * 512:(st + 1) * 512],
                                                                        start=(kc == 0), stop=(kc == KC - 1)),
                                           reads=[ptok, HT_tok[st]], writes=[pstok], signal=(kc == KC - 1))
                                    op(ACT, lambda: nc.scalar.activation(out=uT[:, g, st * 512:(st + 1) * 512], in_=ps[:], func=AF.Gelu_apprx_tanh),
                                       reads=[pstok], writes=[uT_tok[st]], dis=True)
                        else:
                            cc = pn - 4
                            for tt in range(NT):
                                st = tt // 4
                                ps, pstok = next_ps()
                                for kc in range(KC):
                                    op(PE, lambda: nc.tensor.matmul(ps[:], lhsT=HT[:, kc, tt * 128:(tt + 1) * 128], rhs=p3[:, kc, :],
                                                                    start=(kc == 0), stop=(kc == KC - 1)),
                                       reads=[ptok, HT_tok[st]], writes=[pstok], signal=(kc == KC - 1))
                                vf, vftok = vf_rot.next()
                                op(ACT, lambda: nc.scalar.activation(out=vf[:], in_=ps[:], func=AF.Gelu_apprx_tanh), reads=[pstok], writes=[vftok])
                                sq, sqtok = sq_rot.next()
                                op(ACT, lambda: nc.scalar.activation(out=sq[:], in_=vf[:], func=AF.Square, accum_out=sspart[:, tt, cc:cc + 1]),
                                   reads=[vftok], writes=[sqtok, ss_tok], dis=True)
                                op(DVE, lambda: nc.vector.tensor_tensor(out=vb[:, tt, cc * 512:(cc + 1) * 512], in0=vf[:], in1=vgb[:, cc * 512:(cc + 1) * 512], op=ALU.mult),
                                   reads=[vftok, vgb_tok], writes=[vb_tok], dis=True)
                                if pn == 7 and tt == 4:
                                    bg = norm_gen(xsrc, xsrc_toks, blk + 1, gsc1, sh1, vt, HT, HT_tok) if blk + 1 < NB else next_phase_norm(gsc2, sh2, vt)
                                    step(bg)
                    op(DVE, lambda: nc.vector.tensor_reduce(out=ss8[:], in_=sspart[:], axis=AX.X, op=ALU.add), reads=[ss_tok], writes=[ss_tok])
                    op(DVE, lambda: nc.vector.tensor_scalar(out=ss8[:], in0=ss8[:], scalar1=1.0 / D, scalar2=EPS, op0=ALU.mult, op1=ALU.add),
                       reads=[ss_tok], writes=[ss_tok])
                    op(DVE, lambda: nc.vector.reciprocal(out=ss8[:], in_=ss8[:]), reads=[ss_tok], writes=[ss_tok])
                    op(ACT, lambda: nc.scalar.activation(out=ss8[:], in_=ss8[:], func=AF.Sqrt), reads=[ss_tok], writes=[ss_tok])
                    for tt in range(NT):
                        st = tt // 4
                        wp, wptok = wp_rot.next()
                        op(DVE, lambda: nc.vector.tensor_scalar(out=wp[:], in0=wsT[:], scalar1=ss8[:, tt:tt + 1], scalar2=None, op0=ALU.mult),
                           reads=[wsT_tok, ss_tok], writes=[wptok])
                        for gq in range(4):
                            ps, pstok = next_ps()
                            for gi in range(4):
                                g = gq * 4 + gi
                                gs = slice(g * 128, (g + 1) * 128)
                                o_ = ps[:, gi * 128:(gi + 1) * 128]
                                op(PE, lambda: nc.tensor.matmul(o_, lhsT=vb[:, tt, gs], rhs=wp[:, gs], start=True, stop=False),
                                   reads=[vb_tok, wptok], writes=[pstok], signal=False)
                                op(PE, lambda: nc.tensor.matmul(o_, lhsT=ones[0:1, 0:128], rhs=bhi[0:1, gs], start=False, stop=False),
                                   reads=[bs_tok], writes=[pstok], signal=False)
                                op(PE, lambda: nc.tensor.matmul(o_, lhsT=ones[0:1, 0:128], rhs=blo[0:1, gs], start=False, stop=True),
                                   reads=[bs_tok], writes=[pstok], signal=(gi == 3))
                            uv = uT[:, gq * 4:(gq + 1) * 4, tt * 128:(tt + 1) * 128]
                            op(DVE, lambda: nc.vector.tensor_tensor(out=uv, in0=uv, in1=ps[:].rearrange("p (a b) -> p a b", a=4), op=ALU.mult),
                               reads=[pstok, uT_tok[st]], writes=[uT_tok[st]], dis=True)
                            if gq == 0:
                                step(bg)
                    out_proj(ws, blk * 12 + 8, blk, uT, uT_tok, KC, g1c, vt, xsrc, xsrc_toks, bg=bg, bg_n=2)
                    exhaust(bg)
                K.barrier()
                xrot_cur[0] = xt_rot
        xsrc, xsrc_toks = XTL
        if stop_after == ("mix", li):
            break

        with ExitStack() as ph:
            M = ph.enter_context(sbt("M", [128, FC, TB], BF16))
            M_tok = [[GTok("M%d_%d" % (i, g_)) for g_ in range((FC + 7) // 8)] for i in range(NST)]
            NWP = 3
            slots = [(ph.enter_context(sbt("fpan%d" % i, [128, 16 * 512], BF16)), GTok("pan%d" % i)) for i in range(NWP)]
            cw = ph.enter_context(sbt("cw", [128, 3 * 2 * FC], F32))
            cb = ph.enter_context(sbt("cb", [128, 2 * FC], F32))
            carry = ph.enter_context(sbt("carry", [128, 2 * FC, 2], F32))
            CV = GTok("convvec")
            carry_tok = GTok("carry")
            cf_rot = Rot([(ph.enter_context(sbt("cf%d" % i, [128, 512], F32)), GTok("cf%d" % i)) for i in range(4)])
            gs_rot = Rot([(ph.enter_context(sbt("gs%d" % i, [128, 512], F32)), GTok("gs%d" % i)) for i in range(2)])
            dma(SP, cw[:], f_cw[li], CV, writes=[CV])
            dma(SP, cb[:], f_cb[li], CV, writes=[CV], dis=True)
            op(DVE, lambda: nc.vector.memset(carry[:], 0.0), writes=[carry_tok])
            wvi = wview(f_win[li])
            wvo = wview(f_wout[li])
            steps = [(c, min(2, FC - c)) for c in range(0, FC, 2)]

            def mk_a(stp):
                def f(wp, wptok):
                    c0, nch = stp
                    w = nch * 128
                    g3 = wp[:, 0:16 * 256].rearrange("p (k n) -> p k n", k=16)
                    u3 = wp[:, 16 * 256:32 * 256].rearrange("p (k n) -> p k n", k=16)
                    dma(POOL, g3[:, :, 0:w], wvi[:, :, c0 * 128:c0 * 128 + w], wptok, writes=[wptok])
                    dma(POOL, u3[:, :, 0:w], wvi[:, :, DFF + c0 * 128:DFF + c0 * 128 + w], wptok, writes=[wptok], dis=True)
                    return g3, u3
                return f

            def mk_b(n):
                def f(wp, wptok):
                    p3 = wp[:, 0:FC * 128].rearrange("p (k n) -> p k n", k=FC)
                    dma(POOL, p3[:, 0:16, :], wvo[:, 0:16, n * 128:(n + 1) * 128], wptok, writes=[wptok])
                    dma(POOL, p3[:, 16:32, :], wvo[:, 16:32, n * 128:(n + 1) * 128], wptok, writes=[wptok], dis=True)
                    dma(POOL, p3[:, 32:FC, :], wvo[:, 32:FC, n * 128:(n + 1) * 128], wptok, writes=[wptok], dis=True)
                    return p3
                return f

            loads = []
            for blk in range(NB):
                loads += [mk_a(stp) for stp in steps]
                loads += [mk_b(n) for n in range(KC)]
            per_blk = len(steps) + KC
            ws = WStream(slots, loads)
            ws.get(0)

            def conv(ps, pstok, cidx):
                cf, cftok = cf_rot.next()
                w0 = cw[:, cidx:cidx + 1]
                w1 = cw[:, 2 * FC + cidx:2 * FC + cidx + 1]
                w2 = cw[:, 4 * FC + cidx:4 * FC + cidx + 1]
                op(ACT, lambda: nc.scalar.activation(out=cf[:], in_=ps[:], func=AF.Identity, scale=w2, bias=cb[:, cidx:cidx + 1]),
                   reads=[pstok, CV], writes=[cftok])
                op(DVE, lambda: nc.vector.scalar_tensor_tensor(out=cf[:, 1:512], in0=ps[:, 0:511], scalar=w1, in1=cf[:, 1:512], op0=ALU.mult, op1=ALU.add),
                   reads=[pstok, cftok, CV], writes=[cftok])
                op(DVE, lambda: nc.vector.scalar_tensor_tensor(out=cf[:, 2:512], in0=ps[:, 0:510], scalar=w0, in1=cf[:, 2:512], op0=ALU.mult, op1=ALU.add),
                   reads=[pstok, cftok, CV], writes=[cftok])
                op(DVE, lambda: nc.vector.scalar_tensor_tensor(out=cf[:, 0:2], in0=carry[:, cidx, 0:2], scalar=w0, in1=cf[:, 0:2], op0=ALU.mult, op1=ALU.add),
                   reads=[carry_tok, cftok, CV], writes=[cftok])
                op(DVE, lambda: nc.vector.scalar_tensor_tensor(out=cf[:, 0:1], in0=carry[:, cidx, 1:2], scalar=w1, in1=cf[:, 0:1], op0=ALU.mult, op1=ALU.add),
                   reads=[carry_tok, cftok, CV], writes=[cftok])
                op(ACT, lambda: nc.scalar.activation(out=carry[:, cidx, 0:2], in_=ps[:, 510:512], func=AF.Copy),
                   reads=[pstok, carry_tok], writes=[carry_tok])
                return cf, cftok

            first_norm(gsc2, sh2, vt)
            for blk in range(NB):
                bg = None
                for si, stp in enumerate(steps):
                    if si == len(steps) - 1:
                        if blk + 1 < NB:
                            bg = norm_gen(xsrc, xsrc_toks, blk + 1, gsc2, sh2, vt, HT, HT_tok)
                        elif li + 1 < n_layers and (li + 1) in mod_done and stop_after is None:
                            l2 = li + 1
                            bg = next_phase_norm(gsc1all[:, l2 * KC:(l2 + 1) * KC], modall[:, l2 * 96:l2 * 96 + 16], VECL[l2])
                        step(bg)
                    (g3, u3), wptok = ws.get(blk * per_blk + si)
                    c0, nch = stp
                    for ci in range(nch):
                        c = c0 + ci
                        for st in range(NST):
                            psg, psgtok = next_ps()
                            psu, psutok = next_ps()
                            for kc in range(KC):
                                op(PE, lambda: nc.tensor.matmul(psg[:], lhsT=g3[:, kc, ci * 128:(ci + 1) * 128], rhs=HT[:, kc, st * 512:(st + 1) * 512],
                                                                start=(kc == 0), stop=(kc == KC - 1)),
                                   reads=[wptok, HT_tok[st]], writes=[psgtok], signal=(kc == KC - 1))
                            for kc in range(KC):
                                op(PE, lambda: nc.tensor.matmul(psu[:], lhsT=u3[:, kc, ci * 128:(ci + 1) * 128], rhs=HT[:, kc, st * 512:(st + 1) * 512],
                                                                start=(kc == 0), stop=(kc == KC - 1)),
                                   reads=[wptok, HT_tok[st]], writes=[psutok], signal=(kc == KC - 1))
                            cg, cgtok = conv(psg, psgtok, c)
                            cu, cutok = conv(psu, psutok, FC + c)
                            gs_, gstok = gs_rot.next()
                            op(ACT, lambda: nc.scalar.activation(out=gs_[:], in_=cg[:], func=AF.Silu), reads=[cgtok], writes=[gstok])
                            op(DVE, lambda: nc.vector.tensor_tensor(out=M[:, c, st * 512:(st + 1) * 512], in0=gs_[:], in1=cu[:], op=ALU.mult),
                               reads=[gstok, cutok], writes=[M_tok[st][c // 8]], dis=True)
                groups = [(n, st) for n in range(KC) for st in range(NST)]
                nxt = resid_load(0, blk * NST, xsrc, xsrc_toks)
                for gi, (n, st) in enumerate(groups):
                    p3, wptok = ws.get(blk * per_blk + len(steps) + n)
                    cur = nxt
                    if gi + 1 < len(groups):
                        nxt = resid_load(groups[gi + 1][0], blk * NST + groups[gi + 1][1], xsrc, xsrc_toks)
                    ps, pstok = next_ps()
                    for kc in range(FC):
                        op(PE, lambda: nc.tensor.matmul(ps[:], lhsT=p3[:, kc, :], rhs=M[:, kc, st * 512:(st + 1) * 512],
                                                        start=(kc == 0), stop=(kc == FC - 1)),
                           reads=[wptok, M_tok[st][kc // 8]], writes=[pstok], signal=(kc == FC - 1))
                    resid_apply(ps, pstok, cur, n, blk * NST + st, g2c[:, n:n + 1], vt)
                    step(bg, 2)
                exhaust(bg)
            K.barrier()
        if stop_after == ("ffn", li):
            break

    if do_final:
        with ExitStack() as ph:
            frot = Rot(list(nx_rot.items) + [(ph.enter_context(sbt("nxf%d" % i, [128, 512], F32)), GTok("nxf%d" % i)) for i in range(8)])
            for blk in range(NB):
                exhaust(norm_gen(xsrc, xsrc_toks, blk, fing_sb, None, None, to_out=True, rot=frot, PF=6))
            K.barrier()
    else:
        for sg in range(S // 512):
            for c in range(KC):
                xt, xtok = xt_rot.next()
                dma(SP, xt[:], xsrc[c * 128:(c + 1) * 128, sg * 512:(sg + 1) * 512], xtok, reads=[xsrc_toks[sg]], writes=[xtok])
                dma(SP, outT[c * 128:(c + 1) * 128, sg * 512:(sg + 1) * 512], xt[:], xtok, reads=[xtok])
    K.barrier()
    K.es.close()
    return nc


def _consts():
    ident = np.eye(128, dtype=np.float32)
    p = np.arange(128)[:, None]
    cidx = np.arange(512)[None, :]
    masks = np.zeros((128, 4, 512), np.float32)
    for jj in range(4):
        masks[:, jj, :] = np.where(cidx - p >= 128 * jj, 0.0, -30000.0)
    t = np.arange(128)[None, :]
    tri = (t >= p).astype(np.float32)
    tri = np.tile(tri[:, None, :], (1, H, 1)).reshape(128, H * 128)
    return ident, masks.reshape(128, 4 * 512), np.ascontiguousarray(tri)


def prep_inputs(inputs):
    f = lambda a: np.ascontiguousarray(np.asarray(a, dtype=np.float32))
    x = f(inputs["x"])
    c = f(inputs["c"])
    ident, masks, tri = _consts()
    shared = {
        "mod_w": f(inputs["mod_w"]),
        "mod_b": f(np.asarray(inputs["mod_b"]).reshape(4, 96, 128).transpose(2, 0, 1).reshape(128, 4 * 96)),
        "mixg": f(np.asarray(inputs["mix_norm_g"]).reshape(4, KC, 128).transpose(2, 0, 1).reshape(128, 4 * KC)),
        "ffng": f(np.asarray(inputs["ffn_norm_g"]).reshape(4, KC, 128).transpose(2, 0, 1).reshape(128, 4 * KC)),
        "a_win": f(inputs["attn_w_in"]),
        "a_bf": f(np.asarray(inputs["attn_b_f"]).reshape(2, H, 1)),
        "a_wo": f(inputs["attn_w_o"]),
        "g_win": f(inputs["gm_w_in"]),
        "g_vg": f(np.asarray(inputs["gm_v_g"]).reshape(2, 1, D)),
        "g_ws": f(np.asarray(inputs["gm_w_s"]).transpose(0, 3, 1, 2).reshape(2, 128, H * 128)),
        "g_bs": f(np.asarray(inputs["gm_b_s"]).reshape(2, 1, D)),
        "g_wo": f(inputs["gm_w_o"]),
        "f_win": f(inputs["ffn_w_in"]),
        "f_cw": f(np.asarray(inputs["ffn_conv_w"]).reshape(4, 3, 2 * FC, 128).transpose(0, 3, 1, 2).reshape(4, 128, 3 * 2 * FC)),
        "f_cb": f(np.asarray(inputs["ffn_conv_b"]).reshape(4, 2 * FC, 128).transpose(0, 2, 1)),
        "f_wout": f(inputs["ffn_w_out"]),
        "fin_g": f(np.asarray(inputs["final_g"]).reshape(KC, 128).T),
        "c_ident": ident, "c_masks": masks, "c_tri": tri,
    }
    in_maps = []
    for b in range(NCORES):
        m = dict(shared)
        m["xT"] = np.ascontiguousarray(x[b].T)
        m["c_t"] = np.ascontiguousarray(c[b].reshape(KC, 128).T)
        in_maps.append(m)
    return in_maps


def kernel(**inputs):
    in_maps = prep_inputs(inputs)
    nc = build()
    res = run_bass_kernel_spmd(nc, in_maps, core_ids=list(range(NCORES)))
    out = np.stack([np.ascontiguousarray(np.asarray(r["outT"]).T) for r in res.results], axis=0)
    return out.astype(np.float32)
```
